# Optimizing a Trainium2 kernel written in Bass

```python
import jax, jax.numpy as jnp
from jax import lax
import numpy as np

D_MODEL = 2048
BATCH = 8
SEQ = 2048
DEPTH = 2

GRID_W = 64
EPS = 1e-6
CONV_CH = 512
CONV_WIDTH = 31
CONV_PAD = (CONV_WIDTH - 1) // 2
MLA_HEADS = 8
MLA_Q_RANK = 512
MLA_KV_RANK = 256
MLA_NOPE = 128
MLA_ROPE = 64
MLA_V = 128
ROPE_THETA = 10000.0
Q_BLOCK = 128
NA_HEADS = 8
NA_HEAD_DIM = 64
NA_ROWS_MAX = 8
NA_COLS = 16
MIX_CONV = CONV_CH
MIX_MLA = MLA_HEADS * MLA_V
MIX_NA = NA_HEADS * NA_HEAD_DIM
MIX_WIDTH = MIX_CONV + MIX_MLA + MIX_NA
IN_SPLIT_SIZES = (2 * CONV_CH, MLA_Q_RANK, MLA_KV_RANK, MLA_ROPE, MIX_NA, MIX_NA, MIX_NA)
IN_COLS = int(sum(IN_SPLIT_SIZES))
IN_SPLIT_POINTS = tuple(int(v) for v in np.cumsum(IN_SPLIT_SIZES)[:-1])
D_FF = 4 * D_MODEL

kernel_name = "hybrid_conv_mla_natten_encoder"


def rms_norm(x, g):
    xf = x.astype(jnp.float32)
    y = xf * lax.rsqrt(jnp.mean(xf * xf, axis=-1, keepdims=True) + EPS)
    return (y * g.astype(jnp.float32)).astype(x.dtype)


def layer_norm(x, g, b):
    xf = x.astype(jnp.float32)
    mu = jnp.mean(xf, axis=-1, keepdims=True)
    xc = xf - mu
    var = jnp.mean(xc * xc, axis=-1, keepdims=True)
    y = xc * lax.rsqrt(var + EPS) * g.astype(jnp.float32) + b.astype(jnp.float32)
    return y.astype(x.dtype)


def rope_tables(seq_len):
    pos = jnp.arange(seq_len, dtype=jnp.float32)
    inv_freq = 1.0 / (ROPE_THETA ** (jnp.arange(0, MLA_ROPE, 2, dtype=jnp.float32) / MLA_ROPE))
    ang = pos[:, None] * inv_freq[None, :]
    return jnp.cos(ang), jnp.sin(ang)


def apply_rope(x, cos, sin):
    cos = cos.astype(x.dtype)
    sin = sin.astype(x.dtype)
    x1, x2 = jnp.split(x, 2, axis=-1)
    return jnp.concatenate([x1 * cos - x2 * sin, x2 * cos + x1 * sin], axis=-1)


def conformer_conv(u, conv_w, conv_b, ln_g, ln_b):
    a, gate = jnp.split(u, 2, axis=-1)
    h = a * jax.nn.sigmoid(gate)
    h = lax.conv_general_dilated(
        h, conv_w[:, None, :].astype(h.dtype), window_strides=(1,),
        padding=[(CONV_PAD, CONV_PAD)], dimension_numbers=("NWC", "WIO", "NWC"),
        feature_group_count=CONV_CH)
    h = h + conv_b
    h = layer_norm(h, ln_g, ln_b)
    return jax.nn.silu(h)


def mla_attention(c_q, c_kv, k_rope_in, g_q_a, w_uq, g_kv_a, w_ukv, cos, sin):
    B, S, _ = c_q.shape
    q = (rms_norm(c_q, g_q_a) @ w_uq).reshape(B, S, MLA_HEADS, MLA_NOPE + MLA_ROPE)
    q_nope, q_pe = q[..., :MLA_NOPE], q[..., MLA_NOPE:]
    q_pe = apply_rope(q_pe, cos[:, None, :], sin[:, None, :])
    kv = (rms_norm(c_kv, g_kv_a) @ w_ukv).reshape(B, S, MLA_HEADS, MLA_NOPE + MLA_V)
    k_nope, v = kv[..., :MLA_NOPE], kv[..., MLA_NOPE:]
    k_pe = apply_rope(k_rope_in, cos, sin)
    scale = (MLA_NOPE + MLA_ROPE) ** -0.5
    nb = S // Q_BLOCK
    qn_b = q_nope.reshape(B, nb, Q_BLOCK, MLA_HEADS, MLA_NOPE).transpose(1, 0, 2, 3, 4)
    qp_b = q_pe.reshape(B, nb, Q_BLOCK, MLA_HEADS, MLA_ROPE).transpose(1, 0, 2, 3, 4)

    def block(args):
        qn, qp = args
        s = (jnp.einsum("bqhd,bkhd->bhqk", qn, k_nope, preferred_element_type=jnp.float32)
             + jnp.einsum("bqhd,bkd->bhqk", qp, k_pe, preferred_element_type=jnp.float32)) * scale
        p = jax.nn.softmax(s, axis=-1)
        return jnp.einsum("bhqk,bkhd->bqhd", p.astype(v.dtype), v)

    o = lax.map(block, (qn_b, qp_b))
    return o.transpose(1, 0, 2, 3, 4).reshape(B, S, MIX_MLA)


def neighbourhood_attention(q, k, v, rpb):
    B, S, _ = q.shape
    rows = S // GRID_W
    kr = min(NA_ROWS_MAX, rows)
    kc = NA_COLS
    qg = q.reshape(B, rows, GRID_W, NA_HEADS, NA_HEAD_DIM)
    kg = k.reshape(B, rows, GRID_W, NA_HEADS, NA_HEAD_DIM)
    vg = v.reshape(B, rows, GRID_W, NA_HEADS, NA_HEAD_DIM)
    cols = jnp.arange(GRID_W)
    col_start = jnp.clip(cols - kc // 2, 0, GRID_W - kc)
    col_idx = col_start[:, None] + jnp.arange(kc)[None, :]
    dc = col_idx - cols[:, None] + (NA_COLS - 1)
    scale = NA_HEAD_DIM ** -0.5

    def row_block(r):
        r0 = jnp.clip(r - kr // 2, 0, rows - kr)
        k_strip = lax.dynamic_slice_in_dim(kg, r0, kr, axis=1)
        v_strip = lax.dynamic_slice_in_dim(vg, r0, kr, axis=1)
        k_win = k_strip[:, :, col_idx]
        v_win = v_strip[:, :, col_idx]
        q_row = lax.dynamic_index_in_dim(qg, r, axis=1, keepdims=False)
        s = jnp.einsum("bwhd,brwkhd->bhwrk", q_row, k_win,
                       preferred_element_type=jnp.float32) * scale
        dr = r0 + jnp.arange(kr) - r + (NA_ROWS_MAX - 1)
        bias = rpb[:, dr[:, None, None], dc[None, :, :]]
        s = s + bias.transpose(0, 2, 1, 3)[None].astype(jnp.float32)
        p = jax.nn.softmax(s.reshape(B, NA_HEADS, GRID_W, kr * kc), axis=-1)
        p = p.reshape(B, NA_HEADS, GRID_W, kr, kc).astype(v.dtype)
        return jnp.einsum("bhwrk,brwkhd->bwhd", p, v_win)

    o = lax.map(row_block, jnp.arange(rows))
    return o.transpose(1, 0, 2, 3, 4).reshape(B, S, MIX_NA)


def setup_inputs(seed: int = 0) -> dict:
    key = jax.random.key(seed)
    ks = jax.random.split(key, 21)
    L = DEPTH

    def nrm(k, shape, scale):
        return jax.random.normal(k, shape, jnp.float32) * scale

    def gain(k, shape):
        return 1.0 + 0.02 * jax.random.normal(k, shape, jnp.float32)

    return {
        "x": nrm(ks[0], (BATCH, SEQ, D_MODEL), 1.0),
        "g_pre_mix": gain(ks[1], (L, D_MODEL)),
        "w_in": nrm(ks[2], (L, D_MODEL, IN_COLS), D_MODEL ** -0.5),
        "conv_w": nrm(ks[3], (L, CONV_WIDTH, CONV_CH), CONV_WIDTH ** -0.5),
        "conv_b": nrm(ks[4], (L, CONV_CH), 0.02),
        "conv_ln_g": gain(ks[5], (L, CONV_CH)),
        "conv_ln_b": nrm(ks[6], (L, CONV_CH), 0.02),
        "g_q_a": gain(ks[7], (L, MLA_Q_RANK)),
        "w_uq": nrm(ks[8], (L, MLA_Q_RANK, MLA_HEADS * (MLA_NOPE + MLA_ROPE)), MLA_Q_RANK ** -0.5),
        "g_kv_a": gain(ks[9], (L, MLA_KV_RANK)),
        "w_ukv": nrm(ks[10], (L, MLA_KV_RANK, MLA_HEADS * (MLA_NOPE + MLA_V)), MLA_KV_RANK ** -0.5),
        "na_rpb": nrm(ks[11], (L, NA_HEADS, 2 * NA_ROWS_MAX - 1, 2 * NA_COLS - 1), 0.1),
        "g_out_conv": gain(ks[12], (L, MIX_CONV)),
        "g_out_mla": gain(ks[13], (L, MIX_MLA)),
        "g_out_na": gain(ks[14], (L, MIX_NA)),
        "w_o": nrm(ks[15], (L, MIX_WIDTH, D_MODEL), MIX_WIDTH ** -0.5),
        "g_post_mix": gain(ks[16], (L, D_MODEL)),
        "g_pre_ffn": gain(ks[17], (L, D_MODEL)),
        "w_up": nrm(ks[18], (L, D_MODEL, D_FF), D_MODEL ** -0.5),
        "w_down": nrm(ks[19], (L, D_FF, D_MODEL), D_FF ** -0.5),
        "g_post_ffn": gain(ks[20], (L, D_MODEL)),
    }


def reference(x, g_pre_mix, w_in, conv_w, conv_b, conv_ln_g, conv_ln_b, g_q_a, w_uq,
              g_kv_a, w_ukv, na_rpb, g_out_conv, g_out_mla, g_out_na, w_o, g_post_mix,
              g_pre_ffn, w_up, w_down, g_post_ffn):
    S = x.shape[1]
    cos, sin = rope_tables(S)
    for l in range(DEPTH):
        h = rms_norm(x, g_pre_mix[l])
        proj = h @ w_in[l]
        u_conv, c_q, c_kv, k_rope, q_na, k_na, v_na = jnp.split(proj, IN_SPLIT_POINTS, axis=-1)
        y_conv = conformer_conv(u_conv, conv_w[l], conv_b[l], conv_ln_g[l], conv_ln_b[l])
        y_mla = mla_attention(c_q, c_kv, k_rope, g_q_a[l], w_uq[l], g_kv_a[l], w_ukv[l], cos, sin)
        y_na = neighbourhood_attention(q_na, k_na, v_na, na_rpb[l])
        mixed = jnp.concatenate([rms_norm(y_conv, g_out_conv[l]),
                                 rms_norm(y_mla, g_out_mla[l]),
                                 rms_norm(y_na, g_out_na[l])], axis=-1)
        x = x + rms_norm(mixed @ w_o[l], g_post_mix[l])
        h = rms_norm(x, g_pre_ffn[l])
        f = jnp.square(jax.nn.relu(h @ w_up[l])) @ w_down[l]
        x = x + rms_norm(f, g_post_ffn[l])
    return x
```

```python
import contextlib
import numpy as np
import concourse.bass as bass
import concourse.mybir as mybir
from concourse.alu_op_type import AluOpType as ALU
from concourse.bass_utils import run_bass_kernel_spmd

F32 = mybir.dt.float32
BF16 = mybir.dt.bfloat16
AF = mybir.ActivationFunctionType

D = 2048
S = 2048
DEPTH = 2
DFF = 8192
EPS = 1e-6
NCORES = 8
NFM = 24
VW = 8 * 65
NEG = -30000.0


class Buf:
    __slots__ = ("name", "w", "r")

    def __init__(self, name):
        self.name = name
        self.w = {}
        self.r = {}


class _Op:
    __slots__ = ("eng", "fn", "deps", "stream", "idx", "signal", "dma")


ENGS = ("pe", "act", "dve", "pool", "sp")


class Sched:
    def __init__(self, nc, self_sync=True):
        self.nc = nc
        self.q = {e: [] for e in ENGS}
        self.streams = {}
        self.floor = {}
        self.self_sync = self_sync
        self.all_chans = []
        self.next_chan = 0
        self.sems = {}
        self.cnt = {}
        self.vals = {}
        self.waited = {e: {} for e in ENGS}

    def chan(self, name=None):
        if self.next_chan < len(self.all_chans):
            c = self.all_chans[self.next_chan]
        else:
            c = "c%d" % len(self.all_chans)
            self.all_chans.append(c)
        self.next_chan += 1
        return c

    def op(self, eng, fn, reads=(), writes=(), awrites=(), chan=None):
        o = _Op()
        o.eng, o.fn, o.dma = eng, fn, chan is not None
        deps = {}

        def need(d):
            for s, i in d.items():
                if i > self.floor.get(s, -1) and deps.get(s, -1) < i:
                    deps[s] = i

        for b in reads:
            need(b.w)
        for b in writes:
            need(b.w)
            need(b.r)
        for b in awrites:
            need(b.r)
        stream = chan if chan is not None else eng
        if fn is None:
            deps.pop(eng, None)
            o.deps, o.stream, o.idx, o.signal = deps, None, -1, False
            self.q[eng].append(o)
            for s, i in deps.items():
                self.streams[s][i].signal = True
            return o
        lst = self.streams.setdefault(stream, [])
        if chan is not None and lst:
            need({stream: len(lst) - 1})
        if chan is None and stream in deps:
            if eng == "pe" or not self.self_sync:
                del deps[stream]
        o.deps, o.stream, o.idx, o.signal = deps, stream, len(lst), False
        lst.append(o)
        self.q[eng].append(o)
        for s, i in deps.items():
            self.streams[s][i].signal = True
        i = o.idx
        for b in reads:
            if b.r.get(stream, -1) < i:
                b.r[stream] = i
        for b in writes:
            b.w = {stream: i}
            b.r = {}
        for b in awrites:
            if b.w.get(stream, -1) < i:
                b.w[stream] = i
        return o

    def barrier(self):
        deps = {s: len(l) - 1 for s, l in self.streams.items() if l and len(l) - 1 > self.floor.get(s, -1)}
        for s, i in deps.items():
            self.streams[s][i].signal = True
        for e in ENGS:
            o = _Op()
            o.eng, o.fn, o.dma, o.deps, o.stream, o.idx, o.signal = e, None, False, dict(deps), None, -1, False
            self.q[e].append(o)
        for s, i in deps.items():
            self.floor[s] = i
        self.next_chan = 0

    def check(self):
        done = dict(self.done_floor) if hasattr(self, "done_floor") else {}
        ptr = {e: 0 for e in ENGS}
        while True:
            prog = False
            for e in ENGS:
                q = self.q[e]
                while ptr[e] < len(q):
                    o = q[ptr[e]]
                    if all(done.get(s_, -1) >= i for s_, i in o.deps.items()):
                        if o.fn is not None and done.get(o.stream, -1) < o.idx:
                            done[o.stream] = o.idx
                        ptr[e] += 1
                        prog = True
                    else:
                        break
            if not prog:
                break
        stuck = {e: ptr[e] for e in ENGS if ptr[e] < len(self.q[e])}
        if stuck:
            msg = []
            for e, p in stuck.items():
                o = self.q[e][p]
                msg.append("%s@%d/%d waits %s" % (e, p, len(self.q[e]), {k: (v, done.get(k, -1)) for k, v in o.deps.items() if done.get(k, -1) < v}))
            raise RuntimeError("DEADLOCK: " + "; ".join(msg))
        self.done_floor = done

    def emit(self):
        nc = self.nc
        for s, lst in self.streams.items():
            if s not in self.sems:
                self.sems[s] = nc.alloc_semaphore(name="s_" + s)
                self.cnt[s] = 0
                self.vals[s] = []
            v = self.vals[s]
            c = self.cnt[s]
            for o in lst[len(v):]:
                if o.signal:
                    c += 1
                v.append(c)
            self.cnt[s] = c
        sems, vals = self.sems, self.vals
        self.check()
        with nc.Block() as block:
            def run(ename):
                def body(eng):
                    waited = self.waited[ename]
                    for o in self.q[ename]:
                        for s, i in o.deps.items():
                            mult = 16 if self.streams[s][i].dma else 1
                            val = vals[s][i] * mult
                            if waited.get(s, 0) < val:
                                eng.wait_ge(sems[s], val)
                                waited[s] = val
                        if o.fn is not None:
                            inst = o.fn(eng)
                            if o.signal:
                                inst.then_inc(sems[o.stream], 16 if o.dma else 1)
                    self.q[ename] = []
                return body

            block.tensor(run("pe"))
            block.scalar(run("act"))
            block.vector(run("dve"))
            block.gpsimd(run("pool"))
            block.sync(run("sp"))


class Ctx:
    pass


def _bufs(prefix, n):
    return [Buf("%s%d" % (prefix, i)) for i in range(n)]


def mm(sc, out, lhsT, rhs, start, stop, R, W):
    sc.op("pe", lambda e: e.matmul(out, lhsT=lhsT, rhs=rhs, start=start, stop=stop), reads=R, writes=W)


def tr(sc, C, out, in_, R, W):
    sc.op("pe", lambda e: e.transpose(out=out, in_=in_, identity=C.ident[:]), reads=list(R) + [C.ident_b], writes=W)


def act(sc, out, in_, func, R, W, scale=None, bias=None, accum=None):
    kw = {}
    if scale is not None:
        kw["scale"] = scale
    if bias is not None:
        kw["bias"] = bias
    if accum is not None:
        kw["accum_out"] = accum
    sc.op("act", lambda e: e.activation(out=out, in_=in_, func=func, **kw), reads=R, writes=W)


def tt(sc, eng, out, in0, in1, op, R, W):
    sc.op(eng, lambda e: e.tensor_tensor(out=out, in0=in0, in1=in1, op=op), reads=R, writes=W)


def tsc(sc, eng, out, in0, s1, op0, R, W):
    sc.op(eng, lambda e: e.tensor_scalar(out=out, in0=in0, scalar1=s1, scalar2=None, op0=op0), reads=R, writes=W)


def stt(sc, out, in0, scalar, in1, op0, op1, R, W):
    sc.op("dve", lambda e: e.scalar_tensor_tensor(out=out, in0=in0, scalar=scalar, in1=in1, op0=op0, op1=op1),
          reads=R, writes=W)


def recip(sc, out, in_, R, W):
    sc.op("dve", lambda e: e.reciprocal(out=out, in_=in_), reads=R, writes=W)


def cpy(sc, eng, out, in_, R, W):
    sc.op(eng, lambda e: e.tensor_copy(out=out, in_=in_), reads=R, writes=W)


def mset(sc, eng, ap, val, W):
    sc.op(eng, lambda e: e.memset(ap, val), writes=W)


def dma(sc, q, out, in_, chan, R=(), W=(), AW=()):
    sc.op(q, lambda e: e.dma_start(out=out, in_=in_), reads=R, writes=W, awrites=AW, chan=chan)


def rstd_of(sc, out, ssq, n, R, W):
    act(sc, out, ssq, AF.Sqrt, R, W, scale=1.0 / n, bias=EPS)
    recip(sc, out, out, W, W)


def end_phase(sc):
    sc.barrier()
    sc.emit()


def phase_proj_res(sc, nc, C, *, lhs_kind, w_dram, nfc, g_post_dram, x_in, x_in_buf, x_out, x_out_buf,
                   mixT=None, mixT_buf=None, g_pre_dram=None, w_up_dram=None, tag=""):
    TB = 512
    NTS = TB // 128
    NDG = D // 512
    with contextlib.ExitStack() as es:
        al = lambda *a: es.enter_context(nc.sbuf_tensor(*a))
        ps = lambda *a: es.enter_context(nc.psum_tensor(*a))
        aT = al("aT" + tag, [128, nfc, TB], BF16)
        aT_b = _bufs("aT", nfc)
        WG = 4
        wd = [al("wd%d%s" % (i, tag), [128, WG, 512], BF16) for i in range(3)]
        wd_b = _bufs("wd", 3)
        wd_ch = [sc.chan("wd") for i in range(3)]
        fsb = al("fsb" + tag, [128, NTS, D], F32)
        fsb_b = _bufs("fsb", NTS)
        xt = [al("xt%d%s" % (i, tag), [128, D], F32) for i in range(2)]
        xt_b = _bufs("xt", 2)
        xt_ld = [sc.chan("xl") for i in range(2)]
        xt_st = [sc.chan("xs") for i in range(2)]
        gpost = al("gpost" + tag, [128, D], F32)
        gpost_b = Buf("gpost")
        junk = al("junk" + tag, [128, D], BF16)
        junk_b = Buf("junk")
        stat = al("stat" + tag, [128, 8], F32)
        stat_b = _bufs("stat", 8)
        acc = [ps("acc%d%s" % (i, tag), [128, 512], F32) for i in range(NTS)]
        acc_b = _bufs("acc", NTS)
        ssq0, rstd0, ssq1, rstd1 = stat[:, 0:1], stat[:, 1:2], stat[:, 2:3], stat[:, 3:4]

        dma(sc, "sp", gpost[:], g_post_dram.partition_broadcast(128), sc.chan("gp"), W=[gpost_b])
        if lhs_kind == "ffn":
            hT = al("hT" + tag, [128, 16, TB], BF16)
            hT_b = Buf("hT")
            UG = 2
            wu = [al("wu%d%s" % (i, tag), [128, 16, UG * 128], BF16) for i in range(2)]
            wu_b = _bufs("wu", 2)
            wu_ch = [sc.chan("wu") for i in range(2)]
            gpre = al("gpre" + tag, [128, D], F32)
            gpre_b = Buf("gpre")
            xn = al("xn" + tag, [128, D], BF16)
            xn_b = Buf("xn")
            tp = ps("tp" + tag, [128, 1024], BF16)
            tp_b = Buf("tp")
            up = [ps("up%d%s" % (i, tag), [128, 512], F32) for i in range(2)]
            up_b = _bufs("up", 2)
            dma(sc, "sp", gpre[:], g_pre_dram.partition_broadcast(128), sc.chan("gq"), W=[gpre_b])
        else:
            mx_ch = sc.chan("mx")

        xcnt = [0]

        def load_x(rows):
            i = xcnt[0] % 2
            xcnt[0] += 1
            dma(sc, "sp", xt[i][:], rows, xt_ld[i], R=[x_in_buf], W=[xt_b[i]])
            return i

        wdcnt = 0
        wucnt = 0
        for tb in range(S // TB):
            t0 = tb * TB
            if lhs_kind == "ffn":
                for ts in range(NTS):
                    r0 = t0 + ts * 128
                    i = load_x(x_in[r0:r0 + 128, :])
                    act(sc, junk[:], xt[i][:], AF.Square, [xt_b[i]], [junk_b, stat_b[0]], accum=ssq0)
                    rstd_of(sc, rstd0, ssq0, D, [stat_b[0]], [stat_b[1]])
                    stt(sc, xn[:], xt[i][:], rstd0, gpre[:], ALU.mult, ALU.mult, [xt_b[i], stat_b[1], gpre_b], [xn_b])
                    for half in range(2):
                        for k in range(8):
                            c = half * 8 + k
                            tr(sc, C, tp[:, k * 128:(k + 1) * 128], xn[:, c * 128:(c + 1) * 128], [xn_b], [tp_b])
                        act(sc, hT[:, half * 8:(half + 1) * 8, ts * 128:(ts + 1) * 128],
                            tp[:].rearrange("p (k t) -> p k t", k=8), AF.Copy, [tp_b], [hT_b])
                for fg in range(nfc // UG):
                    j = wucnt % 2
                    wucnt += 1
                    src = w_up_dram[:, fg * UG * 128:(fg + 1) * UG * 128].rearrange("(c p) f -> p c f", p=128)
                    dma(sc, "pool", wu[j][:], src, wu_ch[j], W=[wu_b[j]])
                    for u in range(UG):
                        fc = fg * UG + u
                        pb = fc % 2
                        for c in range(16):
                            mm(sc, up[pb][:], wu[j][:, c, u * 128:(u + 1) * 128], hT[:, c, :], c == 0, c == 15,
                               [wu_b[j], hT_b], [up_b[pb]])
                        act(sc, junk[:, 0:TB], up[pb][:], AF.Relu, [up_b[pb]], [junk_b])
                        tt(sc, "dve", aT[:, fc, :], junk[:, 0:TB], junk[:, 0:TB], ALU.mult, [junk_b], [aT_b[fc]])
            else:
                src = mixT[:, t0:t0 + TB].rearrange("(c p) t -> p c t", p=128)
                dma(sc, "sp", aT[:], src, mx_ch, R=[mixT_buf], W=aT_b)
            for dg in range(NDG):
                for fg in range(nfc // WG):
                    j = wdcnt % 3
                    wdcnt += 1
                    src = w_dram[fg * WG * 128:(fg + 1) * WG * 128, dg * 512:(dg + 1) * 512].rearrange(
                        "(c p) f -> p c f", p=128)
                    dma(sc, "pool", wd[j][:], src, wd_ch[j], W=[wd_b[j]])
                    for u in range(WG):
                        fc = fg * WG + u
                        for ts in range(NTS):
                            mm(sc, acc[ts][:], aT[:, fc, ts * 128:(ts + 1) * 128], wd[j][:, u, :], fc == 0,
                               fc == nfc - 1, [aT_b[fc], wd_b[j]], [acc_b[ts]])
                for ts in range(NTS):
                    act(sc, fsb[:, ts, dg * 512:(dg + 1) * 512], acc[ts][:], AF.Copy, [acc_b[ts]], [fsb_b[ts]])
            for ts in range(NTS):
                r0 = t0 + ts * 128
                i = load_x(x_in[r0:r0 + 128, :])
                act(sc, junk[:], fsb[:, ts, :], AF.Square, [fsb_b[ts]], [junk_b, stat_b[2]], accum=ssq1)
                rstd_of(sc, rstd1, ssq1, D, [stat_b[2]], [stat_b[3]])
                stt(sc, fsb[:, ts, :], fsb[:, ts, :], rstd1, gpost[:], ALU.mult, ALU.mult,
                    [fsb_b[ts], stat_b[3], gpost_b], [fsb_b[ts]])
                tt(sc, "dve", xt[i][:], xt[i][:], fsb[:, ts, :], ALU.add, [fsb_b[ts], xt_b[i]], [xt_b[i]])
                dma(sc, "sp", x_out[r0:r0 + 128, :], xt[i][:], xt_st[i], R=[xt_b[i]], AW=[x_out_buf])
        end_phase(sc)


def phase_norm_T(sc, nc, C, *, y_dram, y_buf, W, g_dram, mixT, mixT_buf, row0, tag=""):
    NC_ = W // 128
    with contextlib.ExitStack() as es:
        al = lambda *a: es.enter_context(nc.sbuf_tensor(*a))
        ps = lambda *a: es.enter_context(nc.psum_tensor(*a))
        yt = [al("yt%d%s" % (i, tag), [128, W], F32) for i in range(2)]
        yt_b = _bufs("yt", 2)
        yt_ch = [sc.chan("yl") for i in range(2)]
        gb = al("gb" + tag, [128, W], F32)
        gb_b = Buf("gb")
        yn = al("yn" + tag, [128, W], BF16)
        yn_b = Buf("yn")
        junk = al("junk" + tag, [128, W], BF16)
        junk_b = Buf("junk")
        stat = al("stat" + tag, [128, 2], F32)
        stat_b = _bufs("st", 2)
        stage = [al("stg%d%s" % (i, tag), [128, NC_, 512], BF16) for i in range(2)]
        stage_b = _bufs("stg", 2)
        stage_ch = [sc.chan("sg") for i in range(2)]
        tp = [ps("tp%d%s" % (i, tag), [128, 1024], BF16) for i in range(2)]
        tp_b = _bufs("tp", 2)
        dma(sc, "sp", gb[:], g_dram.partition_broadcast(128), sc.chan("gb"), W=[gb_b])
        n = 0
        for tb in range(4):
            sg = tb % 2
            for ts in range(4):
                r0 = tb * 512 + ts * 128
                i = n % 2
                n += 1
                dma(sc, "sp", yt[i][:], y_dram[r0:r0 + 128, :], yt_ch[i], R=[y_buf], W=[yt_b[i]])
                act(sc, junk[:], yt[i][:], AF.Square, [yt_b[i]], [junk_b, stat_b[0]], accum=stat[:, 0:1])
                rstd_of(sc, stat[:, 1:2], stat[:, 0:1], W, [stat_b[0]], [stat_b[1]])
                stt(sc, yn[:], yt[i][:], stat[:, 1:2], gb[:], ALU.mult, ALU.mult, [yt_b[i], stat_b[1], gb_b], [yn_b])
                for c in range(NC_):
                    tr(sc, C, tp[i][:, c * 128:(c + 1) * 128], yn[:, c * 128:(c + 1) * 128], [yn_b], [tp_b[i]])
                act(sc, stage[sg][:, :, ts * 128:(ts + 1) * 128],
                    tp[i][:, 0:W].rearrange("p (k t) -> p k t", k=NC_), AF.Copy, [tp_b[i]], [stage_b[sg]])
            dst = mixT[row0:row0 + W, tb * 512:(tb + 1) * 512].rearrange("(c p) t -> p c t", p=128)
            dma(sc, "sp", dst, stage[sg][:], stage_ch[sg], R=[stage_b[sg]], AW=[mixT_buf])
        end_phase(sc)


def phase_inproj(sc, nc, C, T, l, x_in, x_in_buf, tag=""):
    with contextlib.ExitStack() as es:
        al = lambda *a: es.enter_context(nc.sbuf_tensor(*a))
        ps = lambda *a: es.enter_context(nc.psum_tensor(*a))
        hT = al("hT" + tag, [128, 16, S], BF16)
        hT_b = Buf("hT")
        with contextlib.ExitStack() as es1:
            al1 = lambda *a: es1.enter_context(nc.sbuf_tensor(*a))
            ps1 = lambda *a: es1.enter_context(nc.psum_tensor(*a))
            xt = [al1("pxt%d%s" % (i, tag), [128, D], F32) for i in range(2)]
            xt_b = _bufs("xt", 2)
            xt_ch = [sc.chan("xl") for i in range(2)]
            gpre = al1("pgpre" + tag, [128, D], F32)
            gpre_b = Buf("gpre")
            xn = [al1("pxn%d%s" % (i, tag), [128, D], BF16) for i in range(2)]
            xn_b = _bufs("xn", 2)
            junk = al1("pjunk" + tag, [128, D], BF16)
            junk_b = Buf("junk")
            stat = al1("pstat" + tag, [128, 2], F32)
            stat_b = _bufs("st", 2)
            tp = [ps1("ptp%d%s" % (i, tag), [128, 1024], BF16) for i in range(2)]
            tp_b = _bufs("tp", 2)
            dma(sc, "sp", gpre[:], T.g_pre_mix[l].partition_broadcast(128), sc.chan("gq"), W=[gpre_b])
            for tti in range(16):
                i = tti % 2
                r0 = tti * 128
                dma(sc, "sp", xt[i][:], x_in[r0:r0 + 128, :], xt_ch[i], R=[x_in_buf], W=[xt_b[i]])
                act(sc, junk[:], xt[i][:], AF.Square, [xt_b[i]], [junk_b, stat_b[0]], accum=stat[:, 0:1])
                rstd_of(sc, stat[:, 1:2], stat[:, 0:1], D, [stat_b[0]], [stat_b[1]])
                stt(sc, xn[i][:], xt[i][:], stat[:, 1:2], gpre[:], ALU.mult, ALU.mult,
                    [xt_b[i], stat_b[1], gpre_b], [xn_b[i]])
                for half in range(2):
                    for k in range(8):
                        c = half * 8 + k
                        tr(sc, C, tp[half][:, k * 128:(k + 1) * 128], xn[i][:, c * 128:(c + 1) * 128], [xn_b[i]],
                           [tp_b[half]])
                    dst = hT[:, half * 8:(half + 1) * 8, r0:r0 + 128]
                    src = tp[half][:].rearrange("p (k t) -> p k t", k=8)
                    if half:
                        cpy(sc, "dve", dst, src, [tp_b[half]], [hT_b])
                    else:
                        act(sc, dst, src, AF.Copy, [tp_b[half]], [hT_b])
            end_phase(sc)
        wsl = [al("wsl%d%s" % (i, tag), [128, 16, 256], BF16) for i in range(2)]
        wsl_b = _bufs("wsl", 2)
        wsl_ch = [sc.chan("ws") for i in range(2)]
        wv = al("wv" + tag, [128, 16, 512], BF16)
        wv_b = Buf("wv")
        rowf = [al("rowf%d%s" % (i, tag), [128, S], F32) for i in range(2)]
        rowf_b = _bufs("rowf", 2)
        rowf_ch = [sc.chan("rf") for i in range(2)]
        rowh = [al("rowh%d%s" % (i, tag), [128, S], BF16) for i in range(2)]
        rowh_b = _bufs("rowh", 2)
        rowh_ch = [sc.chan("rh") for i in range(2)]
        sgt = al("sgt" + tag, [128, S], F32)
        sgt_b = Buf("sgt")
        glu = al("glu" + tag, [128, S + 32], BF16)
        glu_b = Buf("glu")
        diag = [al("diag%d%s" % (i, tag), [128, 31, 128], BF16) for i in range(2)]
        diag_b = _bufs("diag", 2)
        vrow = [al("vrow%d%s" % (i, tag), [128, 8, 65], BF16) for i in range(2)]
        vrow_b = _bufs("vrow", 2)
        vrow_ch = [sc.chan("vr") for i in range(2)]
        pp = [ps("pp%d%s" % (i, tag), [128, 512], F32) for i in range(3)]
        pp_b = _bufs("pp", 3)
        pc = [ps("pc%d%s" % (i, tag), [128, 512], F32) for i in range(2)]
        pc_b = _bufs("pc", 2)
        pcl = C.pcol[l]

        GOFF = 16
        mset(sc, "dve", glu[:, 0:GOFF], 0.0, [glu_b])
        mset(sc, "dve", glu[:, GOFF + S:S + 32], 0.0, [glu_b])
        for i in range(2):
            mset(sc, "dve", vrow[i][:, :, 64:65], 1.0, [vrow_b[i]])
        dma(sc, "pool", wv[:], T.w_in_v[l].rearrange("(c p) f -> p c f", p=128), sc.chan("wv"), W=[wv_b])

        ppn = 0
        rfn = 0
        rhn = 0
        dests = {}
        for k in range(4):
            dests[8 + k] = ("f", T.cqT, T.cqT_b, k * 128, 128)
        for k in range(2):
            dests[12 + k] = ("f", T.ckvT, T.ckvT_b, k * 128, 128)
        dests[14] = ("f", T.krT, T.krT_b, 0, 64)
        dests[15] = ("f", T.krT, T.krT_b, 64, 64)
        for k in range(4):
            dests[16 + k] = ("h", T.qnaT, T.qnaT_b, k * 128, 128)
            dests[20 + k] = ("h", T.knaT, T.knaT_b, k * 128, 128)
        for cg in range(NFM // 2):
            j = cg % 2
            dma(sc, "pool", wsl[j][:], T.w_in_fm[l][:, cg * 256:(cg + 1) * 256].rearrange("(c p) f -> p c f", p=128),
                wsl_ch[j], W=[wsl_b[j]])
            for u in range(2):
                ch = cg * 2 + u
                kind = None
                if ch < 8:
                    kc_ = ch // 2
                    kind = "gate" if ch % 2 == 0 else "a"
                    if kind == "a":
                        dj = kc_ % 2
                        for tap in range(31):
                            col = 22 + kc_ * 31 + tap
                            tsc(sc, "pool", diag[dj][:, tap, :], C.ident_f[:], pcl[:, col:col + 1], ALU.mult,
                                [C.ident_fb, C.pcol_b], [diag_b[dj]])
                else:
                    dk, ddram, dbuf, drow, dnp = dests[ch]
                    if dk == "f":
                        ri = rfn % 2
                        rfn += 1
                        rtile, rbuf, rch = rowf[ri], rowf_b[ri], rowf_ch[ri]
                    else:
                        ri = rhn % 2
                        rhn += 1
                        rtile, rbuf, rch = rowh[ri], rowh_b[ri], rowh_ch[ri]
                for tg in range(4):
                    p = ppn % 3
                    ppn += 1
                    for c in range(16):
                        mm(sc, pp[p][:], wsl[j][:, c, u * 128:(u + 1) * 128], hT[:, c, tg * 512:(tg + 1) * 512],
                           c == 0, c == 15, [wsl_b[j], hT_b], [pp_b[p]])
                    sl = slice(tg * 512, (tg + 1) * 512)
                    if kind == "gate":
                        act(sc, sgt[:, sl], pp[p][:], AF.Sigmoid, [pp_b[p]], [sgt_b])
                    elif kind == "a":
                        tt(sc, "dve", glu[:, GOFF + tg * 512:GOFF + (tg + 1) * 512], pp[p][:], sgt[:, sl], ALU.mult,
                           [pp_b[p], sgt_b], [glu_b])
                    elif dk == "f":
                        act(sc, rtile[:, sl], pp[p][:], AF.Copy, [pp_b[p]], [rbuf])
                    else:
                        cpy(sc, "dve", rtile[:, sl], pp[p][:], [pp_b[p]], [rbuf])
                if kind == "a":
                    ri = rfn % 2
                    rfn += 1
                    for tg in range(4):
                        q = tg % 2
                        for tap in range(31):
                            o0 = GOFF - 15 + tap + tg * 512
                            mm(sc, pc[q][:], diag[dj][:, tap, :], glu[:, o0:o0 + 512], tap == 0, tap == 30,
                               [diag_b[dj], glu_b], [pc_b[q]])
                        act(sc, rowf[ri][:, tg * 512:(tg + 1) * 512], pc[q][:], AF.Identity, [pc_b[q], C.pcol_b],
                            [rowf_b[ri]], bias=pcl[:, kc_:kc_ + 1], scale=1.0)
                    dma(sc, "sp", T.convT[kc_ * 128:(kc_ + 1) * 128, :], rowf[ri][:], rowf_ch[ri], R=[rowf_b[ri]],
                        AW=[T.convT_b])
                elif kind is None:
                    dma(sc, "sp", ddram[drow:drow + dnp, :], rtile[0:dnp, :], rch, R=[rbuf], AW=[dbuf])
        for tti in range(16):
            p = ppn % 3
            ppn += 1
            i = tti % 2
            for c in range(16):
                mm(sc, pp[p][:], hT[:, c, tti * 128:(tti + 1) * 128], wv[:, c, :], c == 0, c == 15, [hT_b, wv_b],
                   [pp_b[p]])
            act(sc, vrow[i][:, :, 0:64], pp[p][:].rearrange("p (h d) -> p h d", h=8), AF.Copy, [pp_b[p]], [vrow_b[i]])
            dma(sc, "sp", T.vna[tti * 128:(tti + 1) * 128, :], vrow[i][:].rearrange("p h d -> p (h d)"), vrow_ch[i],
                R=[vrow_b[i]], AW=[T.vna_b])
        end_phase(sc)


def phase_conv_post(sc, nc, C, T, l, tag=""):
    with contextlib.ExitStack() as es:
        al = lambda *a: es.enter_context(nc.sbuf_tensor(*a))
        ps = lambda *a: es.enter_context(nc.psum_tensor(*a))
        ct = al("ct" + tag, [128, 4, S], F32)
        ct_b = Buf("ct")
        sq = al("sq" + tag, [128, 4, 512], F32)
        sq_b = Buf("sq")
        s32 = al("s32" + tag, [128, 4, 512], F32)
        s32_b = Buf("s32")
        dd = al("dd" + tag, [128, 512], F32)
        dd_b = Buf("dd")
        mean = al("mean" + tag, [128, 512], F32)
        mean_b = Buf("mean")
        rs = al("rs" + tag, [128, 512], F32)
        rs_b = Buf("rs")
        rs2 = al("rs2" + tag, [128, 512], F32)
        rs2_b = Buf("rs2")
        stage = [al("cst%d%s" % (i, tag), [128, 4, 512], BF16) for i in range(2)]
        stage_b = _bufs("cst", 2)
        stage_ch = [sc.chan("cs") for i in range(2)]
        p_s = ps("p_s" + tag, [128, 512], F32)
        p_q = ps("p_q" + tag, [128, 512], F32)
        p_q2 = ps("p_q2" + tag, [128, 512], F32)
        p_sb, p_qb, p_q2b = Buf("p_s"), Buf("p_q"), Buf("p_q2")
        ones = C.ones_f
        pc = C.pcol[l]
        dma(sc, "sp", ct[:], T.convT.rearrange("(c p) t -> p c t", p=128), sc.chan("ct"), R=[T.convT_b], W=[ct_b])
        for tg in range(4):
            sl = slice(tg * 512, (tg + 1) * 512)
            sg = tg % 2
            act(sc, sq[:], ct[:, :, sl], AF.Square, [ct_b], [sq_b])
            for k in range(4):
                mm(sc, p_s[:], ones[:], ct[:, k, sl], k == 0, k == 3, [C.ones_fb, ct_b], [p_sb])
            for k in range(4):
                mm(sc, p_q[:], ones[:], sq[:, k, :], k == 0, k == 3, [C.ones_fb, sq_b], [p_qb])
            act(sc, mean[:], p_s[:], AF.Copy, [p_sb], [mean_b], scale=1.0 / 512)
            tt(sc, "dve", dd[:], mean[:], mean[:], ALU.mult, [mean_b], [dd_b])
            stt(sc, rs[:], p_q[:], 1.0 / 512, dd[:], ALU.mult, ALU.subtract, [p_qb, dd_b], [rs_b])
            act(sc, rs[:], rs[:], AF.Sqrt, [rs_b], [rs_b], scale=1.0, bias=EPS)
            recip(sc, rs[:], rs[:], [rs_b], [rs_b])
            for k in range(4):
                tt(sc, "dve", dd[:], ct[:, k, sl], mean[:], ALU.subtract, [ct_b, mean_b], [dd_b])
                tt(sc, "dve", dd[:], dd[:], rs[:], ALU.mult, [dd_b, rs_b], [dd_b])
                act(sc, s32[:, k, :], dd[:], AF.Silu, [dd_b, C.pcol_b], [s32_b], scale=pc[:, 4 + k:5 + k],
                    bias=pc[:, 8 + k:9 + k])
            act(sc, sq[:], s32[:], AF.Square, [s32_b], [sq_b])
            for k in range(4):
                mm(sc, p_q2[:], ones[:], sq[:, k, :], k == 0, k == 3, [C.ones_fb, sq_b], [p_q2b])
            act(sc, rs2[:], p_q2[:], AF.Sqrt, [p_q2b], [rs2_b], scale=1.0 / 512, bias=EPS)
            recip(sc, rs2[:], rs2[:], [rs2_b], [rs2_b])
            for k in range(4):
                stt(sc, stage[sg][:, k, :], s32[:, k, :], pc[:, 12 + k:13 + k], rs2[:], ALU.mult, ALU.mult,
                    [s32_b, rs2_b, C.pcol_b], [stage_b[sg]])
            dst = T.mixT[0:512, sl].rearrange("(c p) t -> p c t", p=128)
            dma(sc, "sp", dst, stage[sg][:], stage_ch[sg], R=[stage_b[sg]], AW=[T.mixT_b])
        end_phase(sc)


def phase_mla(sc, nc, C, T, l, tag=""):
    SCALE = 192.0 ** -0.5
    with contextlib.ExitStack() as es:
        al = lambda *a: es.enter_context(nc.sbuf_tensor(*a))
        ps = lambda *a: es.enter_context(nc.psum_tensor(*a))
        cqn = al("cqn" + tag, [128, 4, S], BF16)
        cqn_b = Buf("cqn")
        ckvn = al("ckvn" + tag, [128, 2, S], BF16)
        ckvn_b = Buf("ckvn")
        kpe = al("kpe" + tag, [64, S], BF16)
        kpe_b = Buf("kpe")
        V = al("V" + tag, [128, 16, 8, 129], BF16)
        V_b = Buf("V")
        cos2 = al("cos2" + tag, [64, S], F32)
        sins = al("sins" + tag, [64, S], F32)
        tab_b = Buf("tab")
        wuq = al("wuq" + tag, [128, 4, 2048], BF16)
        wuq_b = Buf("wuq")
        wk = al("wk" + tag, [128, 2, 1024], BF16)
        wk_b = Buf("wk")
        pcol = C.pcol[l]
        dma(sc, "sp", cos2[:], T.cos2T, sc.chan("t1"), W=[tab_b])
        dma(sc, "sp", sins[:], T.sinsT, sc.chan("t2"), W=[tab_b])
        dma(sc, "pool", wuq[:], T.w_uq_l[l].rearrange("(c p) f -> p c f", p=128), sc.chan("wq"), W=[wuq_b])
        dma(sc, "pool", wk[:], T.w_ukv_l[l][:, 0:1024].rearrange("(c p) f -> p c f", p=128), sc.chan("wk"), W=[wk_b])
        mset(sc, "dve", V[:, :, :, 128:129], 1.0, [V_b])
        with contextlib.ExitStack() as es1:
            al1 = lambda *a: es1.enter_context(nc.sbuf_tensor(*a))
            ps1 = lambda *a: es1.enter_context(nc.psum_tensor(*a))
            cq = al1("cq" + tag, [128, 4, S], F32)
            cq_b = Buf("cq")
            ckv = al1("ckv" + tag, [128, 2, S], F32)
            ckv_b = Buf("ckv")
            kr = al1("kr" + tag, [64, S], F32)
            krr = al1("krr" + tag, [64, S], F32)
            kr_b, krr_b = Buf("kr"), Buf("krr")
            rq = al1("rq" + tag, [128, S], F32)
            rq_b = Buf("rq")
            rkv = al1("rkv" + tag, [128, S], F32)
            rkv_b = Buf("rkv")
            sq = al1("msq" + tag, [128, 4, 512], F32)
            sq_b = Buf("sq")
            wvv = al1("wvv" + tag, [128, 2, 1024], BF16)
            wvv_b = Buf("wvv")
            pa = [ps1("mpa%d%s" % (i, tag), [128, 512], F32) for i in range(2)]
            pa_b = _bufs("pa", 2)
            pv = [ps1("mpv%d%s" % (i, tag), [128, 512], F32) for i in range(2)]
            pv_b = _bufs("pv", 2)
            dma(sc, "sp", cq[:], T.cqT.rearrange("(c p) t -> p c t", p=128), sc.chan("cq"), R=[T.cqT_b], W=[cq_b])
            dma(sc, "sp", ckv[:], T.ckvT.rearrange("(c p) t -> p c t", p=128), sc.chan("ck"), R=[T.ckvT_b], W=[ckv_b])
            dma(sc, "sp", kr[:], T.krT[0:64, :], sc.chan("kr"), R=[T.krT_b], W=[kr_b])
            dma(sc, "sp", krr[:], T.krT[64:128, :], sc.chan("kq"), R=[T.krT_b], W=[krr_b])
            dma(sc, "pool", wvv[:], T.w_ukv_l[l][:, 1024:2048].rearrange("(c p) f -> p c f", p=128), sc.chan("wv"),
                W=[wvv_b])
            for tg in range(4):
                sl = slice(tg * 512, (tg + 1) * 512)
                act(sc, sq[:], cq[:, :, sl], AF.Square, [cq_b], [sq_b])
                for k in range(4):
                    mm(sc, pa[0][:], C.ones_f[:], sq[:, k, :], k == 0, k == 3, [C.ones_fb, sq_b], [pa_b[0]])
                act(sc, rq[:, sl], pa[0][:], AF.Sqrt, [pa_b[0]], [rq_b], scale=1.0 / 512, bias=EPS)
                act(sc, sq[:, 0:2, :], ckv[:, :, sl], AF.Square, [ckv_b], [sq_b])
                for k in range(2):
                    mm(sc, pa[1][:], C.ones_f[:], sq[:, k, :], k == 0, k == 1, [C.ones_fb, sq_b], [pa_b[1]])
                act(sc, rkv[:, sl], pa[1][:], AF.Sqrt, [pa_b[1]], [rkv_b], scale=1.0 / 256, bias=EPS)
            recip(sc, rq[:], rq[:], [rq_b], [rq_b])
            recip(sc, rkv[:], rkv[:], [rkv_b], [rkv_b])
            for k in range(4):
                stt(sc, cqn[:, k, :], cq[:, k, :], pcol[:, 16 + k:17 + k], rq[:], ALU.mult, ALU.mult,
                    [cq_b, rq_b, C.pcol_b], [cqn_b])
            for k in range(2):
                stt(sc, ckvn[:, k, :], ckv[:, k, :], pcol[:, 20 + k:21 + k], rkv[:], ALU.mult, ALU.mult,
                    [ckv_b, rkv_b, C.pcol_b], [ckvn_b])
            tt(sc, "dve", kr[:], kr[:], cos2[:], ALU.mult, [kr_b, tab_b], [kr_b])
            tt(sc, "dve", krr[:], krr[:], sins[:], ALU.mult, [krr_b, tab_b], [krr_b])
            tt(sc, "dve", kpe[:], kr[:], krr[:], ALU.add, [kr_b, krr_b], [kpe_b])
            n = 0
            for tti in range(16):
                for half in range(2):
                    p = n % 2
                    n += 1
                    for ic in range(2):
                        mm(sc, pv[p][:], ckvn[:, ic, tti * 128:(tti + 1) * 128], wvv[:, ic, half * 512:(half + 1) * 512],
                           ic == 0, ic == 1, [ckvn_b, wvv_b], [pv_b[p]])
                    act(sc, V[:, tti, half * 4:(half + 1) * 4, 0:128], pv[p][:].rearrange("p (h d) -> p h d", h=4),
                        AF.Copy, [pv_b[p]], [V_b])
            end_phase(sc)
        qn = [al("qn%d%s" % (i, tag), [128, S], BF16) for i in range(2)]
        qn_b = _bufs("qn", 2)
        qpe = [al("qpe%d%s" % (i, tag), [64, S], BF16) for i in range(2)]
        qpe_b = _bufs("qpe", 2)
        kn = [al("kn%d%s" % (i, tag), [128, S], BF16) for i in range(2)]
        kn_b = _bufs("kn", 2)
        t1 = al("t1" + tag, [64, 512], F32)
        t2 = al("t2" + tag, [64, 512], F32)
        t1_b, t2_b = Buf("t1"), Buf("t2")
        pT = [al("pT%d%s" % (i, tag), [128, 16, 512], BF16) for i in range(2)]
        pT_b = [_bufs("pT%d_" % i, 16) for i in range(2)]
        ysb = [al("ysb%d%s" % (i, tag), [128, 4, 128], F32) for i in range(2)]
        ysb_b = _bufs("ysb", 2)
        ysb_ch = [sc.chan("ys") for i in range(2)]
        rsum = al("rsum" + tag, [128, 4], F32)
        rsum_b = _bufs("rsum", 4)
        pj = [ps("pj%d%s" % (i, tag), [128, 512], F32) for i in range(2)]
        pj_b = _bufs("pj", 2)
        pst = [ps("pst%d%s" % (i, tag), [128, 512], F32) for i in range(2)]
        pst_b = _bufs("pst", 2)
        po = [ps("po%d%s" % (i, tag), [128, 512], F32) for i in range(2)]
        po_b = _bufs("po", 2)
        cnt = {"pj": 0, "pst": 0, "po": 0, "ys": 0}

        def proj(h):
            s_ = h % 2
            for tg in range(4):
                sl = slice(tg * 512, (tg + 1) * 512)
                p = cnt["pj"] % 2
                cnt["pj"] += 1
                for ic in range(4):
                    mm(sc, pj[p][:], wuq[:, ic, h * 256:h * 256 + 128], cqn[:, ic, sl], ic == 0, ic == 3,
                       [wuq_b, cqn_b], [pj_b[p]])
                cpy(sc, "dve", qn[s_][:, sl], pj[p][:], [pj_b[p]], [qn_b[s_]])
                p = cnt["pj"] % 2
                cnt["pj"] += 1
                for ic in range(2):
                    mm(sc, pj[p][:], wk[:, ic, h * 128:(h + 1) * 128], ckvn[:, ic, sl], ic == 0, ic == 1,
                       [wk_b, ckvn_b], [pj_b[p]])
                act(sc, kn[s_][:, sl], pj[p][:], AF.Copy, [pj_b[p]], [kn_b[s_]])
                p = cnt["pj"] % 2
                cnt["pj"] += 1
                for ic in range(4):
                    mm(sc, pj[p][0:64, :], wuq[:, ic, h * 256 + 128:h * 256 + 192], cqn[:, ic, sl], ic == 0, ic == 3,
                       [wuq_b, cqn_b], [pj_b[p]])
                tt(sc, "dve", t1[:], pj[p][0:64, :], cos2[:, sl], ALU.mult, [pj_b[p], tab_b], [t1_b])
                p = cnt["pj"] % 2
                cnt["pj"] += 1
                for ic in range(4):
                    mm(sc, pj[p][0:64, :], wuq[:, ic, h * 256 + 192:h * 256 + 256], cqn[:, ic, sl], ic == 0, ic == 3,
                       [wuq_b, cqn_b], [pj_b[p]])
                tt(sc, "dve", t2[:], pj[p][0:64, :], sins[:, sl], ALU.mult, [pj_b[p], tab_b], [t2_b])
                tt(sc, "dve", qpe[s_][:, sl], t1[:], t2[:], ALU.add, [t1_b, t2_b], [qpe_b[s_]])

        def scores(h, qg, slot):
            s_ = h % 2
            qs_ = slice(qg * 512, (qg + 1) * 512)
            for kc in range(16):
                p = cnt["pst"] % 2
                cnt["pst"] += 1
                ks = slice(kc * 128, (kc + 1) * 128)
                mm(sc, pst[p][:], kn[s_][:, ks], qn[s_][:, qs_], True, False, [kn_b[s_], qn_b[s_]], [pst_b[p]])
                mm(sc, pst[p][:], kpe[:, ks], qpe[s_][:, qs_], False, True, [kpe_b, qpe_b[s_]], [pst_b[p]])
                act(sc, pT[slot][:, kc, :], pst[p][:], AF.Exp, [pst_b[p]], [pT_b[slot][kc]], scale=SCALE)

        def pvmul(h, qg, slot):
            yi = cnt["ys"] % 2
            cnt["ys"] += 1
            for qs in range(4):
                p = cnt["po"] % 2
                cnt["po"] += 1
                for kc in range(16):
                    mm(sc, po[p][:, 0:129], pT[slot][:, kc, qs * 128:(qs + 1) * 128], V[:, kc, h, :], kc == 0, kc == 15,
                       [pT_b[slot][kc], V_b], [po_b[p]])
                recip(sc, rsum[:, qs:qs + 1], po[p][:, 128:129], [po_b[p]], [rsum_b[qs]])
                act(sc, ysb[yi][:, qs, :], po[p][:, 0:128], AF.Copy, [po_b[p], rsum_b[qs]], [ysb_b[yi]],
                    scale=rsum[:, qs:qs + 1])
            dst = T.ymla[qg * 512:(qg + 1) * 512, h * 128:(h + 1) * 128].rearrange("(q p) d -> p q d", p=128)
            dma(sc, "sp", dst, ysb[yi][:], ysb_ch[yi], R=[ysb_b[yi]], AW=[T.ymla_b])

        units = [(h, qg) for h in range(8) for qg in range(4)]
        proj(0)
        scores(0, 0, 0)
        for ui, (h, qg) in enumerate(units):
            if qg == 0 and h + 1 < 8:
                proj(h + 1)
            if ui + 1 < len(units):
                nh, nq = units[ui + 1]
                scores(nh, nq, (ui + 1) % 2)
            pvmul(h, qg, ui % 2)
        end_phase(sc)


def phase_na(sc, nc, C, T, l, tag=""):
    with contextlib.ExitStack() as es:
        al = lambda *a: es.enter_context(nc.sbuf_tensor(*a))
        ps = lambda *a: es.enter_context(nc.psum_tensor(*a))
        qT = al("nqT" + tag, [128, 4, S], BF16)
        kT = al("nkT" + tag, [128, 4, S], BF16)
        qT_b, kT_b = Buf("qT"), Buf("kT")
        Ve = al("Ve" + tag, [128, 16, VW], BF16)
        Vo = al("Vo" + tag, [128, 15, VW], BF16)
        Ve_b, Vo_b = Buf("Ve"), Buf("Vo")
        EB = al("EB" + tag, [128, 8, 14, 64], BF16)
        EB_b = Buf("EB")
        dma(sc, "sp", qT[:], T.qnaT.rearrange("(c p) t -> p c t", p=128), sc.chan("nq"), R=[T.qnaT_b], W=[qT_b])
        dma(sc, "sp", kT[:], T.knaT.rearrange("(c p) t -> p c t", p=128), sc.chan("nk"), R=[T.knaT_b], W=[kT_b])
        dma(sc, "sp", Ve[:], T.vna.rearrange("(m p) f -> p m f", p=128), sc.chan("ve"), R=[T.vna_b], W=[Ve_b])
        dma(sc, "sp", Vo[:], T.vna[64:64 + 15 * 128, :].rearrange("(m p) f -> p m f", p=128), sc.chan("vo"),
            R=[T.vna_b], W=[Vo_b])
        with contextlib.ExitStack() as es1:
            bt = es1.enter_context(nc.sbuf_tensor("nbt" + tag, [128, 8 * 14 * 64], F32))
            bt_b = Buf("bt")
            dma(sc, "sp", bt[:], T.na_bt[l], sc.chan("bt"), W=[bt_b])
            act(sc, EB[:].rearrange("p h s w -> p (h s w)"), bt[:], AF.Exp, [bt_b], [EB_b])
            end_phase(sc)
        ET = [al("ET%d%s" % (i, tag), [128, 8, 4, 64], BF16) for i in range(2)]
        ET_b = [_bufs("ET%d_" % i, 4) for i in range(2)]
        PT = [al("PT%d%s" % (i, tag), [128, 8, 4, 64], BF16) for i in range(2)]
        PT_b = [_bufs("PT%d_" % i, 4) for i in range(2)]
        yrow = [al("yrow%d%s" % (i, tag), [64, 8, 64], F32) for i in range(2)]
        yrow_b = _bufs("yrow", 2)
        yrow_ch = [sc.chan("yr") for i in range(2)]
        rs = [al("nrs%d%s" % (i, tag), [64, 8], F32) for i in range(2)]
        rs_b = _bufs("nrs", 2)
        pS = [ps("nps%d%s" % (i, tag), [128, 512], F32) for i in range(4)]
        pS_b = _bufs("nps", 4)
        pO = [ps("npo%d%s" % (i, tag), [64, 4, 65], F32) for i in range(4)]
        pO_b = _bufs("npo", 4)
        for r in range(32):
            i = r % 2
            r0 = min(max(r - 4, 0), 24)
            dr0 = r0 - r + 7
            if r0 % 2 == 0:
                Vt, Vb, m0 = Ve, Ve_b, r0 // 2
            else:
                Vt, Vb, m0 = Vo, Vo_b, (r0 - 1) // 2
            qs_ = slice(64 * r, 64 * r + 64)
            for bank in range(4):
                par, g = bank // 2, bank % 2
                heads = (par + 4 * g, par + 4 * g + 2)
                for hh, h in enumerate(heads):
                    c, pb = h // 2, (h % 2) * 64
                    for j in range(4):
                        tok0 = 64 * (r0 + 2 * j)
                        mm(sc, pS[bank][:, (hh * 4 + j) * 64:(hh * 4 + j + 1) * 64], kT[pb:pb + 64, c, tok0:tok0 + 128],
                           qT[pb:pb + 64, c, qs_], True, True, [kT_b, qT_b], [pS_b[bank]])
                act(sc, ET[i][:, bank * 2:bank * 2 + 2, :, :].rearrange("p h j w -> p (h j w)"), pS[bank][:], AF.Exp,
                    [pS_b[bank]], [ET_b[i][bank]], scale=0.125)
                tt(sc, "dve", PT[i][:, bank * 2:bank * 2 + 2, :, :], ET[i][:, bank * 2:bank * 2 + 2, :, :],
                   EB[:, heads[0]:heads[0] + 3:2, dr0:dr0 + 7:2, :], ALU.mult, [ET_b[i][bank], EB_b], [PT_b[i][bank]])
            for half in range(2):
                ob = (r % 2) * 2 + half
                for hh in range(4):
                    h = half * 4 + hh
                    bank = (h % 2) * 2 + h // 4
                    slot = bank * 2 + (h % 4) // 2
                    for j in range(4):
                        mm(sc, pO[ob][:, hh, :], PT[i][:, slot, j, :], Vt[:, m0 + j, h * 65:(h + 1) * 65], j == 0, j == 3,
                           [PT_b[i][bank], Vb], [pO_b[ob]])
                recip(sc, rs[i][:, half * 4:(half + 1) * 4], pO[ob][:, :, 64], [pO_b[ob]], [rs_b[i]])
                tt(sc, "dve", yrow[i][:, half * 4:(half + 1) * 4, :], pO[ob][:, :, 0:64],
                   rs[i][:, half * 4:(half + 1) * 4].unsqueeze(2).to_broadcast([64, 4, 64]), ALU.mult,
                   [pO_b[ob], rs_b[i]], [yrow_b[i]])
            dma(sc, "sp", T.yna[64 * r:64 * r + 64, :], yrow[i][:].rearrange("p h d -> p (h d)"), yrow_ch[i],
                R=[yrow_b[i]], AW=[T.yna_b])
        end_phase(sc)


PHASES_ALL = ("inproj", "convpost", "mla", "mlaT", "na", "naT", "oproj", "ffn")


def build_program(layers=(0, 1), phases=PHASES_ALL, dbg=False):
    nc = bass.Bass("TRN2", target_bir_lowering=False, dynamic_dma_scratch_size=8192)
    sc = Sched(nc)
    C = Ctx()
    T = Ctx()
    L = DEPTH
    dtn = lambda name, shape, dtype=F32, kind="ExternalInput": nc.dram_tensor(name, shape, dtype, kind=kind).ap()
    skind = "ExternalOutput" if dbg else "Internal"
    x = dtn("x", [S, D])
    ident_d = dtn("ident", [128, 128])
    pcol_d = dtn("pcol", [L, 128, 146])
    T.g_pre_mix = dtn("g_pre_mix", [L, D])
    T.w_in_fm = dtn("w_in_fm", [L, D, NFM * 128])
    T.w_in_v = dtn("w_in_v", [L, D, 512])
    T.w_uq_l = dtn("w_uq_l", [L, 512, 2048])
    T.w_ukv_l = dtn("w_ukv_l", [L, 256, 2048])
    T.cos2T = dtn("cos2T", [64, S])
    T.sinsT = dtn("sinsT", [64, S])
    T.na_bt = dtn("na_bt", [L, 128, 8 * 14 * 64])
    g_out_mla = dtn("g_out_mla", [L, 1024])
    g_out_na = dtn("g_out_na", [L, 512])
    w_o = dtn("w_o", [L, D, D]) if "oproj" in phases else None
    g_post_mix = dtn("g_post_mix", [L, D])
    g_pre_ffn = dtn("g_pre_ffn", [L, D])
    w_up = dtn("w_up", [L, D, DFF]) if "ffn" in phases else None
    w_down = dtn("w_down", [L, DFF, D]) if "ffn" in phases else None
    g_post_ffn = dtn("g_post_ffn", [L, D])
    out = dtn("out", [S, D], kind="ExternalOutput")
    T.convT = dtn("s_convT", [512, S], F32, skind)
    T.cqT = dtn("s_cqT", [512, S], F32, skind)
    T.ckvT = dtn("s_ckvT", [256, S], F32, skind)
    T.krT = dtn("s_krT", [128, S], F32, skind)
    T.qnaT = dtn("s_qnaT", [512, S], BF16, skind)
    T.knaT = dtn("s_knaT", [512, S], BF16, skind)
    T.vna = dtn("s_vna", [S, VW], BF16, skind)
    T.ymla = dtn("s_ymla", [S, 1024], F32, skind)
    T.yna = dtn("s_yna", [S, 512], F32, skind)
    T.mixT = dtn("s_mixT", [D, S], BF16, skind)
    x1 = dtn("s_x1", [S, D], F32, skind)
    xmid = dtn("s_xmid", [S, D], F32, skind)
    for n_ in ("convT", "cqT", "ckvT", "krT", "qnaT", "knaT", "vna", "ymla", "yna", "mixT"):
        setattr(T, n_ + "_b", Buf(n_))
    x_b, x1_b, xmid_b, out_b = Buf("x"), Buf("x1"), Buf("xmid"), Buf("out")

    with contextlib.ExitStack() as es0:
        al0 = lambda *a: es0.enter_context(nc.sbuf_tensor(*a))
        C.ident = al0("ident_sb", [128, 128], BF16)
        C.ident_b = Buf("ident")
        C.ident_f = al0("ident_f", [128, 128], F32)
        C.ident_fb = Buf("identf")
        C.ones_f = al0("ones_f", [128, 128], F32)
        C.ones_fb = Buf("onesf")
        pcol_sb = al0("pcol_sb", [128, L, 146], F32)
        C.pcol = [pcol_sb[:, l, :] for l in range(L)]
        C.pcol_b = Buf("pcol")
        dma(sc, "pool", C.ident[:], ident_d, sc.chan("id"), W=[C.ident_b])
        dma(sc, "sp", C.ident_f[:], ident_d, sc.chan("if"), W=[C.ident_fb])
        dma(sc, "sp", pcol_sb[:], pcol_d.rearrange("l p c -> p l c"), sc.chan("pc"), W=[C.pcol_b])
        mset(sc, "dve", C.ones_f[:], 1.0, [C.ones_fb])
        end_phase(sc)
        for l in layers:
            xin, xin_b = (x, x_b) if l == 0 else (xmid, xmid_b)
            xo, xo_b = (xmid, xmid_b) if l == 0 and len(layers) > 1 else (out, out_b)
            tg_ = "L%d" % l
            if "inproj" in phases:
                phase_inproj(sc, nc, C, T, l, xin, xin_b, tag="i" + tg_)
            if "convpost" in phases:
                phase_conv_post(sc, nc, C, T, l, tag="c" + tg_)
            if "mla" in phases:
                phase_mla(sc, nc, C, T, l, tag="m" + tg_)
            if "mlaT" in phases:
                phase_norm_T(sc, nc, C, y_dram=T.ymla, y_buf=T.ymla_b, W=1024, g_dram=g_out_mla[l], mixT=T.mixT,
                             mixT_buf=T.mixT_b, row0=512, tag="a" + tg_)
            if "na" in phases:
                phase_na(sc, nc, C, T, l, tag="n" + tg_)
            if "naT" in phases:
                phase_norm_T(sc, nc, C, y_dram=T.yna, y_buf=T.yna_b, W=512, g_dram=g_out_na[l], mixT=T.mixT,
                             mixT_buf=T.mixT_b, row0=1536, tag="b" + tg_)
            if "oproj" in phases:
                phase_proj_res(sc, nc, C, lhs_kind="mix", w_dram=w_o[l], nfc=16, g_post_dram=g_post_mix[l], x_in=xin,
                               x_in_buf=xin_b, x_out=x1, x_out_buf=x1_b, mixT=T.mixT, mixT_buf=T.mixT_b,
                               tag="o" + tg_)
            if "ffn" in phases:
                phase_proj_res(sc, nc, C, lhs_kind="ffn", w_dram=w_down[l], nfc=64, g_post_dram=g_post_ffn[l],
                               x_in=x1, x_in_buf=x1_b, x_out=xo, x_out_buf=xo_b, g_pre_dram=g_pre_ffn[l],
                               w_up_dram=w_up[l], tag="f" + tg_)
    return nc


def _rope_tables():
    pos = np.arange(S, dtype=np.float32)
    inv_freq = (1.0 / (np.float32(10000.0) ** (np.arange(0, 64, 2, dtype=np.float32) / np.float32(64)))).astype(np.float32)
    ang = (pos[:, None] * inv_freq[None, :]).astype(np.float32)
    cos, sin = np.cos(ang).astype(np.float32), np.sin(ang).astype(np.float32)
    cos2T = np.concatenate([cos, cos], axis=1).T.copy()
    sinsT = np.concatenate([-sin, sin], axis=1).T.copy()
    return np.ascontiguousarray(cos2T), np.ascontiguousarray(sinsT)


def prepare_shared(inp):
    L = DEPTH
    f = lambda k: np.asarray(inp[k], dtype=np.float32)
    w_in = f("w_in")
    rot = (np.arange(64) + 32) % 64
    z64 = np.zeros((L, D, 64), np.float32)
    cols = []
    for k in range(4):
        cols.append(w_in[:, :, 512 + k * 128:512 + (k + 1) * 128])
        cols.append(w_in[:, :, k * 128:(k + 1) * 128])
    cols.append(w_in[:, :, 1024:1536])
    cols.append(w_in[:, :, 1536:1792])
    cols += [w_in[:, :, 1792:1856], z64, w_in[:, :, 1792:1856][:, :, rot], z64]
    cols.append(w_in[:, :, 1856:2368])
    cols.append(w_in[:, :, 2368:2880])
    w_in_fm = np.ascontiguousarray(np.concatenate(cols, axis=2))
    assert w_in_fm.shape[2] == NFM * 128
    w_in_v = np.ascontiguousarray(w_in[:, :, 2880:3392])
    w_uq = f("w_uq").reshape(L, 512, 8, 192)
    w_uq_l = np.ascontiguousarray(np.concatenate(
        [w_uq[..., 0:128], w_uq[..., 128:192], w_uq[..., 128:192][..., rot]], axis=3).reshape(L, 512, 2048))
    w_ukv = f("w_ukv").reshape(L, 256, 8, 256)
    w_ukv_l = np.ascontiguousarray(np.concatenate(
        [w_ukv[..., 0:128].reshape(L, 256, 1024), w_ukv[..., 128:256].reshape(L, 256, 1024)], axis=2))

    def colp(v, n):
        return v.reshape(L, n, 128).transpose(0, 2, 1)

    conv_wT = f("conv_w").transpose(0, 2, 1).reshape(L, 4, 128, 31).transpose(0, 2, 1, 3).reshape(L, 128, 124)
    pcol = np.ascontiguousarray(np.concatenate(
        [colp(f("conv_b"), 4), colp(f("conv_ln_g"), 4), colp(f("conv_ln_b"), 4), colp(f("g_out_conv"), 4),
         colp(f("g_q_a"), 4), colp(f("g_kv_a"), 2), conv_wT], axis=2))
    assert pcol.shape == (L, 128, 146)
    rpb = f("na_rpb")
    p = np.arange(128)
    i2, kc = p // 64, p % 64
    w = np.arange(64)
    cs = np.clip(w - 8, 0, 48)
    ds = np.arange(14)
    dr = ds[None, :, None] + i2[:, None, None]
    dc = kc[:, None, None] - w[None, None, :] + 15
    valid = (kc[:, None, None] >= cs[None, None, :]) & (kc[:, None, None] < cs[None, None, :] + 16)
    dcc = np.clip(dc, 0, 30)
    bt = rpb[:, :, dr, dcc]
    bt = np.where(valid[None, None], bt, np.float32(NEG)).astype(np.float32)
    na_bt = np.ascontiguousarray(bt.transpose(0, 2, 1, 3, 4).reshape(L, 128, 8 * 14 * 64))
    cos2T, sinsT = _rope_tables()
    sh = {
        "ident": np.eye(128, dtype=np.float32), "pcol": pcol, "g_pre_mix": f("g_pre_mix"), "w_in_fm": w_in_fm,
        "w_in_v": w_in_v, "w_uq_l": w_uq_l, "w_ukv_l": w_ukv_l, "cos2T": cos2T, "sinsT": sinsT, "na_bt": na_bt,
        "g_out_mla": f("g_out_mla"), "g_out_na": f("g_out_na"), "w_o": f("w_o"), "g_post_mix": f("g_post_mix"),
        "g_pre_ffn": f("g_pre_ffn"), "w_up": f("w_up"), "w_down": f("w_down"), "g_post_ffn": f("g_post_ffn"),
    }
    return sh


_NC_CACHE = {}


def kernel(**inputs):
    x = np.asarray(inputs["x"], dtype=np.float32)
    sh = prepare_shared(inputs)
    if "nc" not in _NC_CACHE:
        _NC_CACHE["nc"] = build_program()
    nc = _NC_CACHE["nc"]
    in_maps = []
    for b in range(NCORES):
        m = dict(sh)
        m["x"] = np.ascontiguousarray(x[b])
        in_maps.append(m)
    res = run_bass_kernel_spmd(nc, in_maps, core_ids=list(range(NCORES)))
    return np.stack([np.asarray(res.results[b]["out"], dtype=np.float32) for b in range(NCORES)], axis=0)
```

```python
import contextlib
import numpy as np
import concourse.bass as bass
import concourse.mybir as mybir
from concourse.alu_op_type import AluOpType as ALU
from concourse.bass_utils import run_bass_kernel_spmd

F32 = mybir.dt.float32
BF16 = mybir.dt.bfloat16
AF = mybir.ActivationFunctionType

D = 2048
S = 2048
DEPTH = 2
DFF = 8192
EPS = 1e-6
NCORES = 8
NFM = 24
VW = 8 * 65
NEG = -30000.0


class Buf:
    __slots__ = ("name", "w", "r")

    def __init__(self, name):
        self.name = name
        self.w = {}
        self.r = {}


class _Op:
    __slots__ = ("eng", "fn", "deps", "stream", "idx", "signal", "dma")


ENGS = ("pe", "act", "dve", "pool", "sp")


class Sched:
    def __init__(self, nc, self_sync=True):
        self.nc = nc
        self.q = {e: [] for e in ENGS}
        self.streams = {}
        self.floor = {}
        self.self_sync = self_sync
        self.all_chans = []
        self.next_chan = 0
        self.sems = {}
        self.cnt = {}
        self.vals = {}
        self.waited = {e: {} for e in ENGS}

    def chan(self, name=None):
        if self.next_chan < len(self.all_chans):
            c = self.all_chans[self.next_chan]
        else:
            c = "c%d" % len(self.all_chans)
            self.all_chans.append(c)
        self.next_chan += 1
        return c

    def op(self, eng, fn, reads=(), writes=(), awrites=(), chan=None):
        o = _Op()
        o.eng, o.fn, o.dma = eng, fn, chan is not None
        deps = {}

        def need(d):
            for s, i in d.items():
                if i > self.floor.get(s, -1) and deps.get(s, -1) < i:
                    deps[s] = i

        for b in reads:
            need(b.w)
        for b in writes:
            need(b.w)
            need(b.r)
        for b in awrites:
            need(b.r)
        stream = chan if chan is not None else eng
        if fn is None:
            deps.pop(eng, None)
            o.deps, o.stream, o.idx, o.signal = deps, None, -1, False
            self.q[eng].append(o)
            for s, i in deps.items():
                self.streams[s][i].signal = True
            return o
        lst = self.streams.setdefault(stream, [])
        if chan is not None and lst:
            need({stream: len(lst) - 1})
        if chan is None and stream in deps:
            if eng == "pe" or not self.self_sync:
                del deps[stream]
        o.deps, o.stream, o.idx, o.signal = deps, stream, len(lst), False
        lst.append(o)
        self.q[eng].append(o)
        for s, i in deps.items():
            self.streams[s][i].signal = True
        i = o.idx
        for b in reads:
            if b.r.get(stream, -1) < i:
                b.r[stream] = i
        for b in writes:
            b.w = {stream: i}
            b.r = {}
        for b in awrites:
            if b.w.get(stream, -1) < i:
                b.w[stream] = i
        return o

    def barrier(self):
        deps = {s: len(l) - 1 for s, l in self.streams.items() if l and len(l) - 1 > self.floor.get(s, -1)}
        for s, i in deps.items():
            self.streams[s][i].signal = True
        for e in ENGS:
            o = _Op()
            o.eng, o.fn, o.dma, o.deps, o.stream, o.idx, o.signal = e, None, False, dict(deps), None, -1, False
            self.q[e].append(o)
        for s, i in deps.items():
            self.floor[s] = i
        self.next_chan = 0

    def check(self):
        done = dict(self.done_floor) if hasattr(self, "done_floor") else {}
        ptr = {e: 0 for e in ENGS}
        while True:
            prog = False
            for e in ENGS:
                q = self.q[e]
                while ptr[e] < len(q):
                    o = q[ptr[e]]
                    if all(done.get(s_, -1) >= i for s_, i in o.deps.items()):
                        if o.fn is not None and done.get(o.stream, -1) < o.idx:
                            done[o.stream] = o.idx
                        ptr[e] += 1
                        prog = True
                    else:
                        break
            if not prog:
                break
        stuck = {e: ptr[e] for e in ENGS if ptr[e] < len(self.q[e])}
        if stuck:
            msg = []
            for e, p in stuck.items():
                o = self.q[e][p]
                msg.append("%s@%d/%d waits %s" % (e, p, len(self.q[e]), {k: (v, done.get(k, -1)) for k, v in o.deps.items() if done.get(k, -1) < v}))
            raise RuntimeError("DEADLOCK: " + "; ".join(msg))
        self.done_floor = done

    def emit(self):
        nc = self.nc
        for s, lst in self.streams.items():
            if s not in self.sems:
                self.sems[s] = nc.alloc_semaphore(name="s_" + s)
                self.cnt[s] = 0
                self.vals[s] = []
            v = self.vals[s]
            c = self.cnt[s]
            for o in lst[len(v):]:
                if o.signal:
                    c += 1
                v.append(c)
            self.cnt[s] = c
        sems, vals = self.sems, self.vals
        self.check()
        with nc.Block() as block:
            def run(ename):
                def body(eng):
                    waited = self.waited[ename]
                    for o in self.q[ename]:
                        for s, i in o.deps.items():
                            mult = 16 if self.streams[s][i].dma else 1
                            val = vals[s][i] * mult
                            if waited.get(s, 0) < val:
                                eng.wait_ge(sems[s], val)
                                waited[s] = val
                        if o.fn is not None:
                            inst = o.fn(eng)
                            if o.signal:
                                inst.then_inc(sems[o.stream], 16 if o.dma else 1)
                    self.q[ename] = []
                return body

            block.tensor(run("pe"))
            block.scalar(run("act"))
            block.vector(run("dve"))
            block.gpsimd(run("pool"))
            block.sync(run("sp"))


class Ctx:
    pass


def _bufs(prefix, n):
    return [Buf("%s%d" % (prefix, i)) for i in range(n)]


def mm(sc, out, lhsT, rhs, start, stop, R, W):
    sc.op("pe", lambda e: e.matmul(out, lhsT=lhsT, rhs=rhs, start=start, stop=stop), reads=R, writes=W)


def tr(sc, C, out, in_, R, W):
    sc.op("pe", lambda e: e.transpose(out=out, in_=in_, identity=C.ident[:]), reads=list(R) + [C.ident_b], writes=W)


def act(sc, out, in_, func, R, W, scale=None, bias=None, accum=None):
    kw = {}
    if scale is not None:
        kw["scale"] = scale
    if bias is not None:
        kw["bias"] = bias
    if accum is not None:
        kw["accum_out"] = accum
    sc.op("act", lambda e: e.activation(out=out, in_=in_, func=func, **kw), reads=R, writes=W)


def tt(sc, eng, out, in0, in1, op, R, W):
    sc.op(eng, lambda e: e.tensor_tensor(out=out, in0=in0, in1=in1, op=op), reads=R, writes=W)


def tsc(sc, eng, out, in0, s1, op0, R, W):
    sc.op(eng, lambda e: e.tensor_scalar(out=out, in0=in0, scalar1=s1, scalar2=None, op0=op0), reads=R, writes=W)


def stt(sc, out, in0, scalar, in1, op0, op1, R, W):
    sc.op("dve", lambda e: e.scalar_tensor_tensor(out=out, in0=in0, scalar=scalar, in1=in1, op0=op0, op1=op1),
          reads=R, writes=W)


def recip(sc, out, in_, R, W):
    sc.op("dve", lambda e: e.reciprocal(out=out, in_=in_), reads=R, writes=W)


def cpy(sc, eng, out, in_, R, W):
    sc.op(eng, lambda e: e.tensor_copy(out=out, in_=in_), reads=R, writes=W)


def mset(sc, eng, ap, val, W):
    sc.op(eng, lambda e: e.memset(ap, val), writes=W)


def dma(sc, q, out, in_, chan, R=(), W=(), AW=()):
    sc.op(q, lambda e: e.dma_start(out=out, in_=in_), reads=R, writes=W, awrites=AW, chan=chan)


def rstd_of(sc, out, ssq, n, R, W):
    act(sc, out, ssq, AF.Sqrt, R, W, scale=1.0 / n, bias=EPS)
    recip(sc, out, out, W, W)


def end_phase(sc):
    sc.barrier()
    sc.emit()


def phase_proj_res(sc, nc, C, *, lhs_kind, w_dram, nfc, g_post_dram, x_in, x_in_buf, x_out, x_out_buf,
                   mixT=None, mixT_buf=None, g_pre_dram=None, w_up_dram=None, tag=""):
    TB = 512
    NTS = TB // 128
    NDG = D // 512
    with contextlib.ExitStack() as es:
        al = lambda *a: es.enter_context(nc.sbuf_tensor(*a))
        ps = lambda *a: es.enter_context(nc.psum_tensor(*a))
        aT = al("aT" + tag, [128, nfc, TB], BF16)
        aT_b = _bufs("aT", nfc)
        WG = 4
        NWD = 5
        wd = [al("wd%d%s" % (i, tag), [128, WG, 512], BF16) for i in range(NWD)]
        wd_b = _bufs("wd", NWD)
        wd_ch = [sc.chan("wd") for i in range(NWD)]
        fsb = al("fsb" + tag, [128, NTS, D], F32)
        fsb_b = _bufs("fsb", NTS)
        xt = [al("xt%d%s" % (i, tag), [128, D], F32) for i in range(2)]
        xt_b = _bufs("xt", 2)
        xt_ld = [sc.chan("xl") for i in range(2)]
        xt_st = [sc.chan("xs") for i in range(2)]
        gpost = al("gpost" + tag, [128, D], F32)
        gpost_b = Buf("gpost")
        junk = al("junk" + tag, [128, D], BF16)
        junk_b = Buf("junk")
        stat = al("stat" + tag, [128, 8], F32)
        stat_b = _bufs("stat", 8)
        acc = [ps("acc%d%s" % (i, tag), [128, 512], F32) for i in range(NTS)]
        acc_b = _bufs("acc", NTS)
        ssq0, rstd0, ssq1, rstd1 = stat[:, 0:1], stat[:, 1:2], stat[:, 2:3], stat[:, 3:4]

        dma(sc, "sp", gpost[:], g_post_dram.partition_broadcast(128), sc.chan("gp"), W=[gpost_b])
        if lhs_kind == "ffn":
            hT = al("hT" + tag, [128, 16, TB], BF16)
            hT_b = Buf("hT")
            UG = 2
            NWU = 4
            wu = [al("wu%d%s" % (i, tag), [128, 16, UG * 128], BF16) for i in range(NWU)]
            wu_b = _bufs("wu", NWU)
            wu_ch = [sc.chan("wu") for i in range(NWU)]
            gpre = al("gpre" + tag, [128, D], F32)
            gpre_b = Buf("gpre")
            xn = [al("xn%d%s" % (i, tag), [128, D], BF16) for i in range(2)]
            xn_b = _bufs("xn", 2)
            tp = [ps("tp%d%s" % (i, tag), [128, 1024], BF16) for i in range(2)]
            tp_b = _bufs("tp", 2)
            up = [ps("up%d%s" % (i, tag), [128, 512], F32) for i in range(2)]
            up_b = _bufs("up", 2)
            dma(sc, "sp", gpre[:], g_pre_dram.partition_broadcast(128), sc.chan("gq"), W=[gpre_b])
        else:
            mx_ch = sc.chan("mx")

        xcnt = [0]

        def load_x(rows):
            i = xcnt[0] % 2
            xcnt[0] += 1
            dma(sc, "sp", xt[i][:], rows, xt_ld[i], R=[x_in_buf], W=[xt_b[i]])
            return i

        wdcnt = 0
        wucnt = 0
        for tb in range(S // TB):
            t0 = tb * TB
            if lhs_kind == "ffn":
                for ts in range(NTS):
                    r0 = t0 + ts * 128
                    i = load_x(x_in[r0:r0 + 128, :])
                    act(sc, junk[:], xt[i][:], AF.Square, [xt_b[i]], [junk_b, stat_b[0]], accum=ssq0)
                    rstd_of(sc, rstd0, ssq0, D, [stat_b[0]], [stat_b[1]])
                    xi = ts % 2
                    stt(sc, xn[xi][:], xt[i][:], rstd0, gpre[:], ALU.mult, ALU.mult, [xt_b[i], stat_b[1], gpre_b],
                        [xn_b[xi]])
                    for half in range(2):
                        for k in range(8):
                            c = half * 8 + k
                            tr(sc, C, tp[half][:, k * 128:(k + 1) * 128], xn[xi][:, c * 128:(c + 1) * 128], [xn_b[xi]],
                               [tp_b[half]])
                        act(sc, hT[:, half * 8:(half + 1) * 8, ts * 128:(ts + 1) * 128],
                            tp[half][:].rearrange("p (k t) -> p k t", k=8), AF.Copy, [tp_b[half]], [hT_b])
                for fg in range(nfc // UG):
                    j = wucnt % NWU
                    wucnt += 1
                    src = w_up_dram[:, fg * UG * 128:(fg + 1) * UG * 128].rearrange("(c p) f -> p c f", p=128)
                    dma(sc, "pool", wu[j][:], src, wu_ch[j], W=[wu_b[j]])
                    for u in range(UG):
                        fc = fg * UG + u
                        pb = fc % 2
                        for c in range(16):
                            mm(sc, up[pb][:], wu[j][:, c, u * 128:(u + 1) * 128], hT[:, c, :], c == 0, c == 15,
                               [wu_b[j], hT_b], [up_b[pb]])
                        act(sc, junk[:, 0:TB], up[pb][:], AF.Relu, [up_b[pb]], [junk_b])
                        tt(sc, "dve", aT[:, fc, :], junk[:, 0:TB], junk[:, 0:TB], ALU.mult, [junk_b], [aT_b[fc]])
            else:
                src = mixT[:, t0:t0 + TB].rearrange("(c p) t -> p c t", p=128)
                dma(sc, "sp", aT[:], src, mx_ch, R=[mixT_buf], W=aT_b)
            for dg in range(NDG):
                for fg in range(nfc // WG):
                    j = wdcnt % NWD
                    wdcnt += 1
                    src = w_dram[fg * WG * 128:(fg + 1) * WG * 128, dg * 512:(dg + 1) * 512].rearrange(
                        "(c p) f -> p c f", p=128)
                    dma(sc, "pool", wd[j][:], src, wd_ch[j], W=[wd_b[j]])
                    for u in range(WG):
                        fc = fg * WG + u
                        for ts in range(NTS):
                            mm(sc, acc[ts][:], aT[:, fc, ts * 128:(ts + 1) * 128], wd[j][:, u, :], fc == 0,
                               fc == nfc - 1, [aT_b[fc], wd_b[j]], [acc_b[ts]])
                for ts in range(NTS):
                    act(sc, fsb[:, ts, dg * 512:(dg + 1) * 512], acc[ts][:], AF.Copy, [acc_b[ts]], [fsb_b[ts]])
            for ts in range(NTS):
                r0 = t0 + ts * 128
                i = load_x(x_in[r0:r0 + 128, :])
                act(sc, junk[:], fsb[:, ts, :], AF.Square, [fsb_b[ts]], [junk_b, stat_b[2]], accum=ssq1)
                rstd_of(sc, rstd1, ssq1, D, [stat_b[2]], [stat_b[3]])
                stt(sc, fsb[:, ts, :], fsb[:, ts, :], rstd1, gpost[:], ALU.mult, ALU.mult,
                    [fsb_b[ts], stat_b[3], gpost_b], [fsb_b[ts]])
                tt(sc, "dve", xt[i][:], xt[i][:], fsb[:, ts, :], ALU.add, [fsb_b[ts], xt_b[i]], [xt_b[i]])
                dma(sc, "sp", x_out[r0:r0 + 128, :], xt[i][:], xt_st[i], R=[xt_b[i]], AW=[x_out_buf])
        end_phase(sc)


def phase_norm_T(sc, nc, C, *, y_dram, y_buf, W, g_dram, mixT, mixT_buf, row0, tag=""):
    NC_ = W // 128
    with contextlib.ExitStack() as es:
        al = lambda *a: es.enter_context(nc.sbuf_tensor(*a))
        ps = lambda *a: es.enter_context(nc.psum_tensor(*a))
        yt = [al("yt%d%s" % (i, tag), [128, W], F32) for i in range(2)]
        yt_b = _bufs("yt", 2)
        yt_ch = [sc.chan("yl") for i in range(2)]
        gb = al("gb" + tag, [128, W], F32)
        gb_b = Buf("gb")
        yn = al("yn" + tag, [128, W], BF16)
        yn_b = Buf("yn")
        junk = al("junk" + tag, [128, W], BF16)
        junk_b = Buf("junk")
        stat = al("stat" + tag, [128, 2], F32)
        stat_b = _bufs("st", 2)
        stage = [al("stg%d%s" % (i, tag), [128, NC_, 512], BF16) for i in range(2)]
        stage_b = _bufs("stg", 2)
        stage_ch = [sc.chan("sg") for i in range(2)]
        tp = [ps("tp%d%s" % (i, tag), [128, 1024], BF16) for i in range(2)]
        tp_b = _bufs("tp", 2)
        dma(sc, "sp", gb[:], g_dram.partition_broadcast(128), sc.chan("gb"), W=[gb_b])
        n = 0
        for tb in range(4):
            sg = tb % 2
            for ts in range(4):
                r0 = tb * 512 + ts * 128
                i = n % 2
                n += 1
                dma(sc, "sp", yt[i][:], y_dram[r0:r0 + 128, :], yt_ch[i], R=[y_buf], W=[yt_b[i]])
                act(sc, junk[:], yt[i][:], AF.Square, [yt_b[i]], [junk_b, stat_b[0]], accum=stat[:, 0:1])
                rstd_of(sc, stat[:, 1:2], stat[:, 0:1], W, [stat_b[0]], [stat_b[1]])
                stt(sc, yn[:], yt[i][:], stat[:, 1:2], gb[:], ALU.mult, ALU.mult, [yt_b[i], stat_b[1], gb_b], [yn_b])
                for c in range(NC_):
                    tr(sc, C, tp[i][:, c * 128:(c + 1) * 128], yn[:, c * 128:(c + 1) * 128], [yn_b], [tp_b[i]])
                act(sc, stage[sg][:, :, ts * 128:(ts + 1) * 128],
                    tp[i][:, 0:W].rearrange("p (k t) -> p k t", k=NC_), AF.Copy, [tp_b[i]], [stage_b[sg]])
            dst = mixT[row0:row0 + W, tb * 512:(tb + 1) * 512].rearrange("(c p) t -> p c t", p=128)
            dma(sc, "sp", dst, stage[sg][:], stage_ch[sg], R=[stage_b[sg]], AW=[mixT_buf])
        end_phase(sc)


def phase_inproj(sc, nc, C, T, l, x_in, x_in_buf, tag=""):
    with contextlib.ExitStack() as es:
        al = lambda *a: es.enter_context(nc.sbuf_tensor(*a))
        ps = lambda *a: es.enter_context(nc.psum_tensor(*a))
        hT = al("hT" + tag, [128, 16, S], BF16)
        hT_b = Buf("hT")
        with contextlib.ExitStack() as es1:
            al1 = lambda *a: es1.enter_context(nc.sbuf_tensor(*a))
            ps1 = lambda *a: es1.enter_context(nc.psum_tensor(*a))
            xt = [al1("pxt%d%s" % (i, tag), [128, D], F32) for i in range(2)]
            xt_b = _bufs("xt", 2)
            xt_ch = [sc.chan("xl") for i in range(2)]
            gpre = al1("pgpre" + tag, [128, D], F32)
            gpre_b = Buf("gpre")
            xn = [al1("pxn%d%s" % (i, tag), [128, D], BF16) for i in range(2)]
            xn_b = _bufs("xn", 2)
            junk = al1("pjunk" + tag, [128, D], BF16)
            junk_b = Buf("junk")
            stat = al1("pstat" + tag, [128, 2], F32)
            stat_b = _bufs("st", 2)
            tp = [ps1("ptp%d%s" % (i, tag), [128, 1024], BF16) for i in range(2)]
            tp_b = _bufs("tp", 2)
            dma(sc, "sp", gpre[:], T.g_pre_mix[l].partition_broadcast(128), sc.chan("gq"), W=[gpre_b])
            for tti in range(16):
                i = tti % 2
                r0 = tti * 128
                dma(sc, "sp", xt[i][:], x_in[r0:r0 + 128, :], xt_ch[i], R=[x_in_buf], W=[xt_b[i]])
                act(sc, junk[:], xt[i][:], AF.Square, [xt_b[i]], [junk_b, stat_b[0]], accum=stat[:, 0:1])
                rstd_of(sc, stat[:, 1:2], stat[:, 0:1], D, [stat_b[0]], [stat_b[1]])
                stt(sc, xn[i][:], xt[i][:], stat[:, 1:2], gpre[:], ALU.mult, ALU.mult,
                    [xt_b[i], stat_b[1], gpre_b], [xn_b[i]])
                for half in range(2):
                    for k in range(8):
                        c = half * 8 + k
                        tr(sc, C, tp[half][:, k * 128:(k + 1) * 128], xn[i][:, c * 128:(c + 1) * 128], [xn_b[i]],
                           [tp_b[half]])
                    dst = hT[:, half * 8:(half + 1) * 8, r0:r0 + 128]
                    src = tp[half][:].rearrange("p (k t) -> p k t", k=8)
                    if half:
                        cpy(sc, "dve", dst, src, [tp_b[half]], [hT_b])
                    else:
                        act(sc, dst, src, AF.Copy, [tp_b[half]], [hT_b])
            end_phase(sc)
        wsl = [al("wsl%d%s" % (i, tag), [128, 16, 256], BF16) for i in range(3)]
        wsl_b = _bufs("wsl", 3)
        wsl_ch = [sc.chan("ws") for i in range(3)]
        wv = al("wv" + tag, [128, 16, 512], BF16)
        wv_b = Buf("wv")
        rowf = [al("rowf%d%s" % (i, tag), [128, S], F32) for i in range(2)]
        rowf_b = _bufs("rowf", 2)
        rowf_ch = [sc.chan("rf") for i in range(2)]
        rowh = [al("rowh%d%s" % (i, tag), [128, S], BF16) for i in range(2)]
        rowh_b = _bufs("rowh", 2)
        rowh_ch = [sc.chan("rh") for i in range(2)]
        sgt = al("sgt" + tag, [128, S], F32)
        sgt_b = Buf("sgt")
        glu = al("glu" + tag, [128, S + 32], BF16)
        glu_b = Buf("glu")
        diag = [al("diag%d%s" % (i, tag), [128, 31, 128], BF16) for i in range(2)]
        diag_b = _bufs("diag", 2)
        vrow = [al("vrow%d%s" % (i, tag), [128, 8, 65], BF16) for i in range(2)]
        vrow_b = _bufs("vrow", 2)
        vrow_ch = [sc.chan("vr") for i in range(2)]
        pp = [ps("pp%d%s" % (i, tag), [128, 512], F32) for i in range(3)]
        pp_b = _bufs("pp", 3)
        pc = [ps("pc%d%s" % (i, tag), [128, 512], F32) for i in range(2)]
        pc_b = _bufs("pc", 2)
        pcl = C.pcol[l]

        GOFF = 16
        mset(sc, "dve", glu[:, 0:GOFF], 0.0, [glu_b])
        mset(sc, "dve", glu[:, GOFF + S:S + 32], 0.0, [glu_b])
        for i in range(2):
            mset(sc, "dve", vrow[i][:, :, 64:65], 1.0, [vrow_b[i]])
        dma(sc, "pool", wv[:], T.w_in_v[l].rearrange("(c p) f -> p c f", p=128), sc.chan("wv"), W=[wv_b])

        ppn = 0
        rfn = 0
        rhn = 0
        dests = {}
        for k in range(4):
            dests[8 + k] = ("f", T.cqT, T.cqT_b, k * 128, 128)
        for k in range(2):
            dests[12 + k] = ("f", T.ckvT, T.ckvT_b, k * 128, 128)
        dests[14] = ("f", T.krT, T.krT_b, 0, 64)
        dests[15] = ("f", T.krT, T.krT_b, 64, 64)
        for k in range(4):
            dests[16 + k] = ("h", T.qnaT, T.qnaT_b, k * 128, 128)
            dests[20 + k] = ("h", T.knaT, T.knaT_b, k * 128, 128)
        for cg in range(NFM // 2):
            j = cg % 3
            dma(sc, "pool", wsl[j][:], T.w_in_fm[l][:, cg * 256:(cg + 1) * 256].rearrange("(c p) f -> p c f", p=128),
                wsl_ch[j], W=[wsl_b[j]])
            for u in range(2):
                ch = cg * 2 + u
                kind = None
                if ch < 8:
                    kc_ = ch // 2
                    kind = "gate" if ch % 2 == 0 else "a"
                    if kind == "a":
                        dj = kc_ % 2
                        for tap in range(31):
                            col = 22 + kc_ * 31 + tap
                            tsc(sc, "pool", diag[dj][:, tap, :], C.ident_f[:], pcl[:, col:col + 1], ALU.mult,
                                [C.ident_fb, C.pcol_b], [diag_b[dj]])
                else:
                    dk, ddram, dbuf, drow, dnp = dests[ch]
                    if dk == "f":
                        ri = rfn % 2
                        rfn += 1
                        rtile, rbuf, rch = rowf[ri], rowf_b[ri], rowf_ch[ri]
                    else:
                        ri = rhn % 2
                        rhn += 1
                        rtile, rbuf, rch = rowh[ri], rowh_b[ri], rowh_ch[ri]
                for tg in range(4):
                    p = ppn % 3
                    ppn += 1
                    for c in range(16):
                        mm(sc, pp[p][:], wsl[j][:, c, u * 128:(u + 1) * 128], hT[:, c, tg * 512:(tg + 1) * 512],
                           c == 0, c == 15, [wsl_b[j], hT_b], [pp_b[p]])
                    sl = slice(tg * 512, (tg + 1) * 512)
                    if kind == "gate":
                        act(sc, sgt[:, sl], pp[p][:], AF.Sigmoid, [pp_b[p]], [sgt_b])
                    elif kind == "a":
                        tt(sc, "dve", glu[:, GOFF + tg * 512:GOFF + (tg + 1) * 512], pp[p][:], sgt[:, sl], ALU.mult,
                           [pp_b[p], sgt_b], [glu_b])
                    elif dk == "f":
                        act(sc, rtile[:, sl], pp[p][:], AF.Copy, [pp_b[p]], [rbuf])
                    else:
                        cpy(sc, "dve", rtile[:, sl], pp[p][:], [pp_b[p]], [rbuf])
                if kind == "a":
                    ri = rfn % 2
                    rfn += 1
                    for tg in range(4):
                        q = tg % 2
                        for tap in range(31):
                            o0 = GOFF - 15 + tap + tg * 512
                            mm(sc, pc[q][:], diag[dj][:, tap, :], glu[:, o0:o0 + 512], tap == 0, tap == 30,
                               [diag_b[dj], glu_b], [pc_b[q]])
                        act(sc, rowf[ri][:, tg * 512:(tg + 1) * 512], pc[q][:], AF.Identity, [pc_b[q], C.pcol_b],
                            [rowf_b[ri]], bias=pcl[:, kc_:kc_ + 1], scale=1.0)
                    dma(sc, "sp", T.convT[kc_ * 128:(kc_ + 1) * 128, :], rowf[ri][:], rowf_ch[ri], R=[rowf_b[ri]],
                        AW=[T.convT_b])
                elif kind is None:
                    dma(sc, "sp", ddram[drow:drow + dnp, :], rtile[0:dnp, :], rch, R=[rbuf], AW=[dbuf])
        for tti in range(16):
            p = ppn % 3
            ppn += 1
            i = tti % 2
            for c in range(16):
                mm(sc, pp[p][:], hT[:, c, tti * 128:(tti + 1) * 128], wv[:, c, :], c == 0, c == 15, [hT_b, wv_b],
                   [pp_b[p]])
            act(sc, vrow[i][:, :, 0:64], pp[p][:].rearrange("p (h d) -> p h d", h=8), AF.Copy, [pp_b[p]], [vrow_b[i]])
            dma(sc, "sp", T.vna[tti * 128:(tti + 1) * 128, :], vrow[i][:].rearrange("p h d -> p (h d)"), vrow_ch[i],
                R=[vrow_b[i]], AW=[T.vna_b])
        end_phase(sc)


def phase_conv_post(sc, nc, C, T, l, tag=""):
    with contextlib.ExitStack() as es:
        al = lambda *a: es.enter_context(nc.sbuf_tensor(*a))
        ps = lambda *a: es.enter_context(nc.psum_tensor(*a))
        ct = al("ct" + tag, [128, 4, S], F32)
        ct_b = Buf("ct")
        sq = al("sq" + tag, [128, 4, 512], F32)
        sq_b = Buf("sq")
        s32 = al("s32" + tag, [128, 4, 512], F32)
        s32_b = Buf("s32")
        dd = al("dd" + tag, [128, 512], F32)
        dd_b = Buf("dd")
        mean = al("mean" + tag, [128, 512], F32)
        mean_b = Buf("mean")
        rs = al("rs" + tag, [128, 512], F32)
        rs_b = Buf("rs")
        rs2 = al("rs2" + tag, [128, 512], F32)
        rs2_b = Buf("rs2")
        stage = [al("cst%d%s" % (i, tag), [128, 4, 512], BF16) for i in range(2)]
        stage_b = _bufs("cst", 2)
        stage_ch = [sc.chan("cs") for i in range(2)]
        p_s = ps("p_s" + tag, [128, 512], F32)
        p_q = ps("p_q" + tag, [128, 512], F32)
        p_q2 = ps("p_q2" + tag, [128, 512], F32)
        p_sb, p_qb, p_q2b = Buf("p_s"), Buf("p_q"), Buf("p_q2")
        ones = C.ones_f
        pc = C.pcol[l]
        dma(sc, "sp", ct[:], T.convT.rearrange("(c p) t -> p c t", p=128), sc.chan("ct"), R=[T.convT_b], W=[ct_b])
        for tg in range(4):
            sl = slice(tg * 512, (tg + 1) * 512)
            sg = tg % 2
            act(sc, sq[:], ct[:, :, sl], AF.Square, [ct_b], [sq_b])
            for k in range(4):
                mm(sc, p_s[:], ones[:], ct[:, k, sl], k == 0, k == 3, [C.ones_fb, ct_b], [p_sb])
            for k in range(4):
                mm(sc, p_q[:], ones[:], sq[:, k, :], k == 0, k == 3, [C.ones_fb, sq_b], [p_qb])
            act(sc, mean[:], p_s[:], AF.Copy, [p_sb], [mean_b], scale=1.0 / 512)
            tt(sc, "dve", dd[:], mean[:], mean[:], ALU.mult, [mean_b], [dd_b])
            stt(sc, rs[:], p_q[:], 1.0 / 512, dd[:], ALU.mult, ALU.subtract, [p_qb, dd_b], [rs_b])
            act(sc, rs[:], rs[:], AF.Sqrt, [rs_b], [rs_b], scale=1.0, bias=EPS)
            recip(sc, rs[:], rs[:], [rs_b], [rs_b])
            for k in range(4):
                tt(sc, "dve", dd[:], ct[:, k, sl], mean[:], ALU.subtract, [ct_b, mean_b], [dd_b])
                tt(sc, "dve", dd[:], dd[:], rs[:], ALU.mult, [dd_b, rs_b], [dd_b])
                act(sc, s32[:, k, :], dd[:], AF.Silu, [dd_b, C.pcol_b], [s32_b], scale=pc[:, 4 + k:5 + k],
                    bias=pc[:, 8 + k:9 + k])
            act(sc, sq[:], s32[:], AF.Square, [s32_b], [sq_b])
            for k in range(4):
                mm(sc, p_q2[:], ones[:], sq[:, k, :], k == 0, k == 3, [C.ones_fb, sq_b], [p_q2b])
            act(sc, rs2[:], p_q2[:], AF.Sqrt, [p_q2b], [rs2_b], scale=1.0 / 512, bias=EPS)
            recip(sc, rs2[:], rs2[:], [rs2_b], [rs2_b])
            for k in range(4):
                stt(sc, stage[sg][:, k, :], s32[:, k, :], pc[:, 12 + k:13 + k], rs2[:], ALU.mult, ALU.mult,
                    [s32_b, rs2_b, C.pcol_b], [stage_b[sg]])
            dst = T.mixT[0:512, sl].rearrange("(c p) t -> p c t", p=128)
            dma(sc, "sp", dst, stage[sg][:], stage_ch[sg], R=[stage_b[sg]], AW=[T.mixT_b])
        end_phase(sc)


def phase_mla(sc, nc, C, T, l, tag=""):
    SCALE = 192.0 ** -0.5
    with contextlib.ExitStack() as es:
        al = lambda *a: es.enter_context(nc.sbuf_tensor(*a))
        ps = lambda *a: es.enter_context(nc.psum_tensor(*a))
        cqn = al("cqn" + tag, [128, 4, S], BF16)
        cqn_b = Buf("cqn")
        ckvn = al("ckvn" + tag, [128, 2, S], BF16)
        ckvn_b = Buf("ckvn")
        kpe = al("kpe" + tag, [64, S], BF16)
        kpe_b = Buf("kpe")
        V = al("V" + tag, [128, 16, 8, 129], BF16)
        V_b = Buf("V")
        cos2 = al("cos2" + tag, [64, S], F32)
        sins = al("sins" + tag, [64, S], F32)
        tab_b = Buf("tab")
        wuq = al("wuq" + tag, [128, 4, 2048], BF16)
        wuq_b = Buf("wuq")
        wk = al("wk" + tag, [128, 2, 1024], BF16)
        wk_b = Buf("wk")
        pcol = C.pcol[l]
        dma(sc, "sp", cos2[:], T.cos2T, sc.chan("t1"), W=[tab_b])
        dma(sc, "sp", sins[:], T.sinsT, sc.chan("t2"), W=[tab_b])
        dma(sc, "pool", wuq[:], T.w_uq_l[l].rearrange("(c p) f -> p c f", p=128), sc.chan("wq"), W=[wuq_b])
        dma(sc, "pool", wk[:], T.w_ukv_l[l][:, 0:1024].rearrange("(c p) f -> p c f", p=128), sc.chan("wk"), W=[wk_b])
        mset(sc, "dve", V[:, :, :, 128:129], 1.0, [V_b])
        with contextlib.ExitStack() as es1:
            al1 = lambda *a: es1.enter_context(nc.sbuf_tensor(*a))
            ps1 = lambda *a: es1.enter_context(nc.psum_tensor(*a))
            cq = al1("cq" + tag, [128, 4, S], F32)
            cq_b = Buf("cq")
            ckv = al1("ckv" + tag, [128, 2, S], F32)
            ckv_b = Buf("ckv")
            kr = al1("kr" + tag, [64, S], F32)
            krr = al1("krr" + tag, [64, S], F32)
            kr_b, krr_b = Buf("kr"), Buf("krr")
            rq = al1("rq" + tag, [128, S], F32)
            rq_b = Buf("rq")
            rkv = al1("rkv" + tag, [128, S], F32)
            rkv_b = Buf("rkv")
            sq = al1("msq" + tag, [128, 4, 512], F32)
            sq_b = Buf("sq")
            wvv = al1("wvv" + tag, [128, 2, 1024], BF16)
            wvv_b = Buf("wvv")
            pa = [ps1("mpa%d%s" % (i, tag), [128, 512], F32) for i in range(2)]
            pa_b = _bufs("pa", 2)
            pv = [ps1("mpv%d%s" % (i, tag), [128, 512], F32) for i in range(2)]
            pv_b = _bufs("pv", 2)
            dma(sc, "sp", cq[:], T.cqT.rearrange("(c p) t -> p c t", p=128), sc.chan("cq"), R=[T.cqT_b], W=[cq_b])
            dma(sc, "sp", ckv[:], T.ckvT.rearrange("(c p) t -> p c t", p=128), sc.chan("ck"), R=[T.ckvT_b], W=[ckv_b])
            dma(sc, "sp", kr[:], T.krT[0:64, :], sc.chan("kr"), R=[T.krT_b], W=[kr_b])
            dma(sc, "sp", krr[:], T.krT[64:128, :], sc.chan("kq"), R=[T.krT_b], W=[krr_b])
            dma(sc, "pool", wvv[:], T.w_ukv_l[l][:, 1024:2048].rearrange("(c p) f -> p c f", p=128), sc.chan("wv"),
                W=[wvv_b])
            for tg in range(4):
                sl = slice(tg * 512, (tg + 1) * 512)
                act(sc, sq[:], cq[:, :, sl], AF.Square, [cq_b], [sq_b])
                for k in range(4):
                    mm(sc, pa[0][:], C.ones_f[:], sq[:, k, :], k == 0, k == 3, [C.ones_fb, sq_b], [pa_b[0]])
                act(sc, rq[:, sl], pa[0][:], AF.Sqrt, [pa_b[0]], [rq_b], scale=1.0 / 512, bias=EPS)
                act(sc, sq[:, 0:2, :], ckv[:, :, sl], AF.Square, [ckv_b], [sq_b])
                for k in range(2):
                    mm(sc, pa[1][:], C.ones_f[:], sq[:, k, :], k == 0, k == 1, [C.ones_fb, sq_b], [pa_b[1]])
                act(sc, rkv[:, sl], pa[1][:], AF.Sqrt, [pa_b[1]], [rkv_b], scale=1.0 / 256, bias=EPS)
            recip(sc, rq[:], rq[:], [rq_b], [rq_b])
            recip(sc, rkv[:], rkv[:], [rkv_b], [rkv_b])
            for k in range(4):
                stt(sc, cqn[:, k, :], cq[:, k, :], pcol[:, 16 + k:17 + k], rq[:], ALU.mult, ALU.mult,
                    [cq_b, rq_b, C.pcol_b], [cqn_b])
            for k in range(2):
                stt(sc, ckvn[:, k, :], ckv[:, k, :], pcol[:, 20 + k:21 + k], rkv[:], ALU.mult, ALU.mult,
                    [ckv_b, rkv_b, C.pcol_b], [ckvn_b])
            tt(sc, "dve", kr[:], kr[:], cos2[:], ALU.mult, [kr_b, tab_b], [kr_b])
            tt(sc, "dve", krr[:], krr[:], sins[:], ALU.mult, [krr_b, tab_b], [krr_b])
            tt(sc, "dve", kpe[:], kr[:], krr[:], ALU.add, [kr_b, krr_b], [kpe_b])
            n = 0
            for tti in range(16):
                for half in range(2):
                    p = n % 2
                    n += 1
                    for ic in range(2):
                        mm(sc, pv[p][:], ckvn[:, ic, tti * 128:(tti + 1) * 128], wvv[:, ic, half * 512:(half + 1) * 512],
                           ic == 0, ic == 1, [ckvn_b, wvv_b], [pv_b[p]])
                    act(sc, V[:, tti, half * 4:(half + 1) * 4, 0:128], pv[p][:].rearrange("p (h d) -> p h d", h=4),
                        AF.Copy, [pv_b[p]], [V_b])
            end_phase(sc)
        qn = [al("qn%d%s" % (i, tag), [128, S], BF16) for i in range(2)]
        qn_b = _bufs("qn", 2)
        qpe = [al("qpe%d%s" % (i, tag), [64, S], BF16) for i in range(2)]
        qpe_b = _bufs("qpe", 2)
        kn = [al("kn%d%s" % (i, tag), [128, S], BF16) for i in range(2)]
        kn_b = _bufs("kn", 2)
        t1 = al("t1" + tag, [64, 512], F32)
        t2 = al("t2" + tag, [64, 512], F32)
        t1_b, t2_b = Buf("t1"), Buf("t2")
        pT = [al("pT%d%s" % (i, tag), [128, 16, 512], BF16) for i in range(2)]
        pT_b = [_bufs("pT%d_" % i, 16) for i in range(2)]
        ysb = [al("ysb%d%s" % (i, tag), [128, 4, 128], F32) for i in range(2)]
        ysb_b = _bufs("ysb", 2)
        ysb_ch = [sc.chan("ys") for i in range(2)]
        rsum = al("rsum" + tag, [128, 4], F32)
        rsum_b = _bufs("rsum", 4)
        pj = [ps("pj%d%s" % (i, tag), [128, 512], F32) for i in range(2)]
        pj_b = _bufs("pj", 2)
        pst = [ps("pst%d%s" % (i, tag), [128, 512], F32) for i in range(2)]
        pst_b = _bufs("pst", 2)
        po = [ps("po%d%s" % (i, tag), [128, 512], F32) for i in range(2)]
        po_b = _bufs("po", 2)
        cnt = {"pj": 0, "pst": 0, "po": 0, "ys": 0}

        def proj(h):
            s_ = h % 2
            for tg in range(4):
                sl = slice(tg * 512, (tg + 1) * 512)
                p = cnt["pj"] % 2
                cnt["pj"] += 1
                for ic in range(4):
                    mm(sc, pj[p][:], wuq[:, ic, h * 256:h * 256 + 128], cqn[:, ic, sl], ic == 0, ic == 3,
                       [wuq_b, cqn_b], [pj_b[p]])
                cpy(sc, "dve", qn[s_][:, sl], pj[p][:], [pj_b[p]], [qn_b[s_]])
                p = cnt["pj"] % 2
                cnt["pj"] += 1
                for ic in range(2):
                    mm(sc, pj[p][:], wk[:, ic, h * 128:(h + 1) * 128], ckvn[:, ic, sl], ic == 0, ic == 1,
                       [wk_b, ckvn_b], [pj_b[p]])
                act(sc, kn[s_][:, sl], pj[p][:], AF.Copy, [pj_b[p]], [kn_b[s_]])
                p = cnt["pj"] % 2
                cnt["pj"] += 1
                for ic in range(4):
                    mm(sc, pj[p][0:64, :], wuq[:, ic, h * 256 + 128:h * 256 + 192], cqn[:, ic, sl], ic == 0, ic == 3,
                       [wuq_b, cqn_b], [pj_b[p]])
                tt(sc, "dve", t1[:], pj[p][0:64, :], cos2[:, sl], ALU.mult, [pj_b[p], tab_b], [t1_b])
                p = cnt["pj"] % 2
                cnt["pj"] += 1
                for ic in range(4):
                    mm(sc, pj[p][0:64, :], wuq[:, ic, h * 256 + 192:h * 256 + 256], cqn[:, ic, sl], ic == 0, ic == 3,
                       [wuq_b, cqn_b], [pj_b[p]])
                tt(sc, "dve", t2[:], pj[p][0:64, :], sins[:, sl], ALU.mult, [pj_b[p], tab_b], [t2_b])
                tt(sc, "dve", qpe[s_][:, sl], t1[:], t2[:], ALU.add, [t1_b, t2_b], [qpe_b[s_]])

        def scores(h, qg, slot):
            s_ = h % 2
            qs_ = slice(qg * 512, (qg + 1) * 512)
            for kc in range(16):
                p = cnt["pst"] % 2
                cnt["pst"] += 1
                ks = slice(kc * 128, (kc + 1) * 128)
                mm(sc, pst[p][:], kn[s_][:, ks], qn[s_][:, qs_], True, False, [kn_b[s_], qn_b[s_]], [pst_b[p]])
                mm(sc, pst[p][:], kpe[:, ks], qpe[s_][:, qs_], False, True, [kpe_b, qpe_b[s_]], [pst_b[p]])
                act(sc, pT[slot][:, kc, :], pst[p][:], AF.Exp, [pst_b[p]], [pT_b[slot][kc]], scale=SCALE)

        def pvmul(h, qg, slot):
            yi = cnt["ys"] % 2
            cnt["ys"] += 1
            for qs in range(4):
                p = cnt["po"] % 2
                cnt["po"] += 1
                for kc in range(16):
                    mm(sc, po[p][:, 0:129], pT[slot][:, kc, qs * 128:(qs + 1) * 128], V[:, kc, h, :], kc == 0, kc == 15,
                       [pT_b[slot][kc], V_b], [po_b[p]])
                recip(sc, rsum[:, qs:qs + 1], po[p][:, 128:129], [po_b[p]], [rsum_b[qs]])
                act(sc, ysb[yi][:, qs, :], po[p][:, 0:128], AF.Copy, [po_b[p], rsum_b[qs]], [ysb_b[yi]],
                    scale=rsum[:, qs:qs + 1])
            dst = T.ymla[qg * 512:(qg + 1) * 512, h * 128:(h + 1) * 128].rearrange("(q p) d -> p q d", p=128)
            dma(sc, "sp", dst, ysb[yi][:], ysb_ch[yi], R=[ysb_b[yi]], AW=[T.ymla_b])

        units = [(h, qg) for h in range(8) for qg in range(4)]
        proj(0)
        scores(0, 0, 0)
        for ui, (h, qg) in enumerate(units):
            if qg == 0 and h + 1 < 8:
                proj(h + 1)
            if ui + 1 < len(units):
                nh, nq = units[ui + 1]
                scores(nh, nq, (ui + 1) % 2)
            pvmul(h, qg, ui % 2)
        end_phase(sc)


def phase_na(sc, nc, C, T, l, tag=""):
    with contextlib.ExitStack() as es:
        al = lambda *a: es.enter_context(nc.sbuf_tensor(*a))
        ps = lambda *a: es.enter_context(nc.psum_tensor(*a))
        qT = al("nqT" + tag, [128, 4, S], BF16)
        kT = al("nkT" + tag, [128, 4, S], BF16)
        qT_b, kT_b = Buf("qT"), Buf("kT")
        Ve = al("Ve" + tag, [128, 16, VW], BF16)
        Vo = al("Vo" + tag, [128, 15, VW], BF16)
        Ve_b, Vo_b = Buf("Ve"), Buf("Vo")
        EB = al("EB" + tag, [128, 8, 14, 64], BF16)
        EB_b = Buf("EB")
        dma(sc, "sp", qT[:], T.qnaT.rearrange("(c p) t -> p c t", p=128), sc.chan("nq"), R=[T.qnaT_b], W=[qT_b])
        dma(sc, "sp", kT[:], T.knaT.rearrange("(c p) t -> p c t", p=128), sc.chan("nk"), R=[T.knaT_b], W=[kT_b])
        dma(sc, "sp", Ve[:], T.vna.rearrange("(m p) f -> p m f", p=128), sc.chan("ve"), R=[T.vna_b], W=[Ve_b])
        dma(sc, "sp", Vo[:], T.vna[64:64 + 15 * 128, :].rearrange("(m p) f -> p m f", p=128), sc.chan("vo"),
            R=[T.vna_b], W=[Vo_b])
        with contextlib.ExitStack() as es1:
            bt = es1.enter_context(nc.sbuf_tensor("nbt" + tag, [128, 8 * 14 * 64], F32))
            bt_b = Buf("bt")
            dma(sc, "sp", bt[:], T.na_bt[l], sc.chan("bt"), W=[bt_b])
            act(sc, EB[:].rearrange("p h s w -> p (h s w)"), bt[:], AF.Exp, [bt_b], [EB_b])
            end_phase(sc)
        ET = [al("ET%d%s" % (i, tag), [128, 8, 4, 64], BF16) for i in range(2)]
        ET_b = [_bufs("ET%d_" % i, 4) for i in range(2)]
        PT = [al("PT%d%s" % (i, tag), [128, 8, 4, 64], BF16) for i in range(2)]
        PT_b = [_bufs("PT%d_" % i, 4) for i in range(2)]
        yrow = [al("yrow%d%s" % (i, tag), [64, 8, 64], F32) for i in range(2)]
        yrow_b = _bufs("yrow", 2)
        yrow_ch = [sc.chan("yr") for i in range(2)]
        rs = [al("nrs%d%s" % (i, tag), [64, 8], F32) for i in range(2)]
        rs_b = _bufs("nrs", 2)
        pS = [ps("nps%d%s" % (i, tag), [128, 512], F32) for i in range(4)]
        pS_b = _bufs("nps", 4)
        pO = [ps("npo%d%s" % (i, tag), [64, 4, 65], F32) for i in range(4)]
        pO_b = _bufs("npo", 4)
        for r in range(32):
            i = r % 2
            r0 = min(max(r - 4, 0), 24)
            dr0 = r0 - r + 7
            if r0 % 2 == 0:
                Vt, Vb, m0 = Ve, Ve_b, r0 // 2
            else:
                Vt, Vb, m0 = Vo, Vo_b, (r0 - 1) // 2
            qs_ = slice(64 * r, 64 * r + 64)
            for bank in range(4):
                par, g = bank // 2, bank % 2
                heads = (par + 4 * g, par + 4 * g + 2)
                for hh, h in enumerate(heads):
                    c, pb = h // 2, (h % 2) * 64
                    for j in range(4):
                        tok0 = 64 * (r0 + 2 * j)
                        mm(sc, pS[bank][:, (hh * 4 + j) * 64:(hh * 4 + j + 1) * 64], kT[pb:pb + 64, c, tok0:tok0 + 128],
                           qT[pb:pb + 64, c, qs_], True, True, [kT_b, qT_b], [pS_b[bank]])
                act(sc, ET[i][:, bank * 2:bank * 2 + 2, :, :].rearrange("p h j w -> p (h j w)"), pS[bank][:], AF.Exp,
                    [pS_b[bank]], [ET_b[i][bank]], scale=0.125)
                tt(sc, "dve", PT[i][:, bank * 2:bank * 2 + 2, :, :], ET[i][:, bank * 2:bank * 2 + 2, :, :],
                   EB[:, heads[0]:heads[0] + 3:2, dr0:dr0 + 7:2, :], ALU.mult, [ET_b[i][bank], EB_b], [PT_b[i][bank]])
            for half in range(2):
                ob = (r % 2) * 2 + half
                for hh in range(4):
                    h = half * 4 + hh
                    bank = (h % 2) * 2 + h // 4
                    slot = bank * 2 + (h % 4) // 2
                    for j in range(4):
                        mm(sc, pO[ob][:, hh, :], PT[i][:, slot, j, :], Vt[:, m0 + j, h * 65:(h + 1) * 65], j == 0, j == 3,
                           [PT_b[i][bank], Vb], [pO_b[ob]])
                recip(sc, rs[i][:, half * 4:(half + 1) * 4], pO[ob][:, :, 64], [pO_b[ob]], [rs_b[i]])
                tt(sc, "dve", yrow[i][:, half * 4:(half + 1) * 4, :], pO[ob][:, :, 0:64],
                   rs[i][:, half * 4:(half + 1) * 4].unsqueeze(2).to_broadcast([64, 4, 64]), ALU.mult,
                   [pO_b[ob], rs_b[i]], [yrow_b[i]])
            dma(sc, "sp", T.yna[64 * r:64 * r + 64, :], yrow[i][:].rearrange("p h d -> p (h d)"), yrow_ch[i],
                R=[yrow_b[i]], AW=[T.yna_b])
        end_phase(sc)


PHASES_ALL = ("inproj", "convpost", "mla", "mlaT", "na", "naT", "oproj", "ffn")


def build_program(layers=(0, 1), phases=PHASES_ALL, dbg=False):
    nc = bass.Bass("TRN2", target_bir_lowering=False, dynamic_dma_scratch_size=8192)
    sc = Sched(nc)
    C = Ctx()
    T = Ctx()
    L = DEPTH
    dtn = lambda name, shape, dtype=F32, kind="ExternalInput": nc.dram_tensor(name, shape, dtype, kind=kind).ap()
    skind = "ExternalOutput" if dbg else "Internal"
    x = dtn("x", [S, D])
    ident_d = dtn("ident", [128, 128])
    pcol_d = dtn("pcol", [L, 128, 146])
    T.g_pre_mix = dtn("g_pre_mix", [L, D])
    T.w_in_fm = dtn("w_in_fm", [L, D, NFM * 128])
    T.w_in_v = dtn("w_in_v", [L, D, 512])
    T.w_uq_l = dtn("w_uq_l", [L, 512, 2048])
    T.w_ukv_l = dtn("w_ukv_l", [L, 256, 2048])
    T.cos2T = dtn("cos2T", [64, S])
    T.sinsT = dtn("sinsT", [64, S])
    T.na_bt = dtn("na_bt", [L, 128, 8 * 14 * 64])
    g_out_mla = dtn("g_out_mla", [L, 1024])
    g_out_na = dtn("g_out_na", [L, 512])
    w_o = dtn("w_o", [L, D, D]) if "oproj" in phases else None
    g_post_mix = dtn("g_post_mix", [L, D])
    g_pre_ffn = dtn("g_pre_ffn", [L, D])
    w_up = dtn("w_up", [L, D, DFF]) if "ffn" in phases else None
    w_down = dtn("w_down", [L, DFF, D]) if "ffn" in phases else None
    g_post_ffn = dtn("g_post_ffn", [L, D])
    out = dtn("out", [S, D], kind="ExternalOutput")
    T.convT = dtn("s_convT", [512, S], F32, skind)
    T.cqT = dtn("s_cqT", [512, S], F32, skind)
    T.ckvT = dtn("s_ckvT", [256, S], F32, skind)
    T.krT = dtn("s_krT", [128, S], F32, skind)
    T.qnaT = dtn("s_qnaT", [512, S], BF16, skind)
    T.knaT = dtn("s_knaT", [512, S], BF16, skind)
    T.vna = dtn("s_vna", [S, VW], BF16, skind)
    T.ymla = dtn("s_ymla", [S, 1024], F32, skind)
    T.yna = dtn("s_yna", [S, 512], F32, skind)
    T.mixT = dtn("s_mixT", [D, S], BF16, skind)
    x1 = dtn("s_x1", [S, D], F32, skind)
    xmid = dtn("s_xmid", [S, D], F32, skind)
    for n_ in ("convT", "cqT", "ckvT", "krT", "qnaT", "knaT", "vna", "ymla", "yna", "mixT"):
        setattr(T, n_ + "_b", Buf(n_))
    x_b, x1_b, xmid_b, out_b = Buf("x"), Buf("x1"), Buf("xmid"), Buf("out")

    with contextlib.ExitStack() as es0:
        al0 = lambda *a: es0.enter_context(nc.sbuf_tensor(*a))
        C.ident = al0("ident_sb", [128, 128], BF16)
        C.ident_b = Buf("ident")
        C.ident_f = al0("ident_f", [128, 128], F32)
        C.ident_fb = Buf("identf")
        C.ones_f = al0("ones_f", [128, 128], F32)
        C.ones_fb = Buf("onesf")
        pcol_sb = al0("pcol_sb", [128, L, 146], F32)
        C.pcol = [pcol_sb[:, l, :] for l in range(L)]
        C.pcol_b = Buf("pcol")
        dma(sc, "pool", C.ident[:], ident_d, sc.chan("id"), W=[C.ident_b])
        dma(sc, "sp", C.ident_f[:], ident_d, sc.chan("if"), W=[C.ident_fb])
        dma(sc, "sp", pcol_sb[:], pcol_d.rearrange("l p c -> p l c"), sc.chan("pc"), W=[C.pcol_b])
        mset(sc, "dve", C.ones_f[:], 1.0, [C.ones_fb])
        end_phase(sc)
        for l in layers:
            xin, xin_b = (x, x_b) if l == 0 else (xmid, xmid_b)
            xo, xo_b = (xmid, xmid_b) if l == 0 and len(layers) > 1 else (out, out_b)
            tg_ = "L%d" % l
            if "inproj" in phases:
                phase_inproj(sc, nc, C, T, l, xin, xin_b, tag="i" + tg_)
            if "convpost" in phases:
                phase_conv_post(sc, nc, C, T, l, tag="c" + tg_)
            if "mla" in phases:
                phase_mla(sc, nc, C, T, l, tag="m" + tg_)
            if "mlaT" in phases:
                phase_norm_T(sc, nc, C, y_dram=T.ymla, y_buf=T.ymla_b, W=1024, g_dram=g_out_mla[l], mixT=T.mixT,
                             mixT_buf=T.mixT_b, row0=512, tag="a" + tg_)
            if "na" in phases:
                phase_na(sc, nc, C, T, l, tag="n" + tg_)
            if "naT" in phases:
                phase_norm_T(sc, nc, C, y_dram=T.yna, y_buf=T.yna_b, W=512, g_dram=g_out_na[l], mixT=T.mixT,
                             mixT_buf=T.mixT_b, row0=1536, tag="b" + tg_)
            if "oproj" in phases:
                phase_proj_res(sc, nc, C, lhs_kind="mix", w_dram=w_o[l], nfc=16, g_post_dram=g_post_mix[l], x_in=xin,
                               x_in_buf=xin_b, x_out=x1, x_out_buf=x1_b, mixT=T.mixT, mixT_buf=T.mixT_b,
                               tag="o" + tg_)
            if "ffn" in phases:
                phase_proj_res(sc, nc, C, lhs_kind="ffn", w_dram=w_down[l], nfc=64, g_post_dram=g_post_ffn[l],
                               x_in=x1, x_in_buf=x1_b, x_out=xo, x_out_buf=xo_b, g_pre_dram=g_pre_ffn[l],
                               w_up_dram=w_up[l], tag="f" + tg_)
    return nc


def _rope_tables():
    pos = np.arange(S, dtype=np.float32)
    inv_freq = (1.0 / (np.float32(10000.0) ** (np.arange(0, 64, 2, dtype=np.float32) / np.float32(64)))).astype(np.float32)
    ang = (pos[:, None] * inv_freq[None, :]).astype(np.float32)
    cos, sin = np.cos(ang).astype(np.float32), np.sin(ang).astype(np.float32)
    cos2T = np.concatenate([cos, cos], axis=1).T.copy()
    sinsT = np.concatenate([-sin, sin], axis=1).T.copy()
    return np.ascontiguousarray(cos2T), np.ascontiguousarray(sinsT)


def prepare_shared(inp):
    L = DEPTH
    f = lambda k: np.asarray(inp[k], dtype=np.float32)
    w_in = f("w_in")
    rot = (np.arange(64) + 32) % 64
    z64 = np.zeros((L, D, 64), np.float32)
    cols = []
    for k in range(4):
        cols.append(w_in[:, :, 512 + k * 128:512 + (k + 1) * 128])
        cols.append(w_in[:, :, k * 128:(k + 1) * 128])
    cols.append(w_in[:, :, 1024:1536])
    cols.append(w_in[:, :, 1536:1792])
    cols += [w_in[:, :, 1792:1856], z64, w_in[:, :, 1792:1856][:, :, rot], z64]
    cols.append(w_in[:, :, 1856:2368])
    cols.append(w_in[:, :, 2368:2880])
    w_in_fm = np.ascontiguousarray(np.concatenate(cols, axis=2))
    assert w_in_fm.shape[2] == NFM * 128
    w_in_v = np.ascontiguousarray(w_in[:, :, 2880:3392])
    w_uq = f("w_uq").reshape(L, 512, 8, 192)
    w_uq_l = np.ascontiguousarray(np.concatenate(
        [w_uq[..., 0:128], w_uq[..., 128:192], w_uq[..., 128:192][..., rot]], axis=3).reshape(L, 512, 2048))
    w_ukv = f("w_ukv").reshape(L, 256, 8, 256)
    w_ukv_l = np.ascontiguousarray(np.concatenate(
        [w_ukv[..., 0:128].reshape(L, 256, 1024), w_ukv[..., 128:256].reshape(L, 256, 1024)], axis=2))

    def colp(v, n):
        return v.reshape(L, n, 128).transpose(0, 2, 1)

    conv_wT = f("conv_w").transpose(0, 2, 1).reshape(L, 4, 128, 31).transpose(0, 2, 1, 3).reshape(L, 128, 124)
    pcol = np.ascontiguousarray(np.concatenate(
        [colp(f("conv_b"), 4), colp(f("conv_ln_g"), 4), colp(f("conv_ln_b"), 4), colp(f("g_out_conv"), 4),
         colp(f("g_q_a"), 4), colp(f("g_kv_a"), 2), conv_wT], axis=2))
    assert pcol.shape == (L, 128, 146)
    rpb = f("na_rpb")
    p = np.arange(128)
    i2, kc = p // 64, p % 64
    w = np.arange(64)
    cs = np.clip(w - 8, 0, 48)
    ds = np.arange(14)
    dr = ds[None, :, None] + i2[:, None, None]
    dc = kc[:, None, None] - w[None, None, :] + 15
    valid = (kc[:, None, None] >= cs[None, None, :]) & (kc[:, None, None] < cs[None, None, :] + 16)
    dcc = np.clip(dc, 0, 30)
    bt = rpb[:, :, dr, dcc]
    bt = np.where(valid[None, None], bt, np.float32(NEG)).astype(np.float32)
    na_bt = np.ascontiguousarray(bt.transpose(0, 2, 1, 3, 4).reshape(L, 128, 8 * 14 * 64))
    cos2T, sinsT = _rope_tables()
    sh = {
        "ident": np.eye(128, dtype=np.float32), "pcol": pcol, "g_pre_mix": f("g_pre_mix"), "w_in_fm": w_in_fm,
        "w_in_v": w_in_v, "w_uq_l": w_uq_l, "w_ukv_l": w_ukv_l, "cos2T": cos2T, "sinsT": sinsT, "na_bt": na_bt,
        "g_out_mla": f("g_out_mla"), "g_out_na": f("g_out_na"), "w_o": f("w_o"), "g_post_mix": f("g_post_mix"),
        "g_pre_ffn": f("g_pre_ffn"), "w_up": f("w_up"), "w_down": f("w_down"), "g_post_ffn": f("g_post_ffn"),
    }
    return sh


_NC_CACHE = {}


def kernel(**inputs):
    x = np.asarray(inputs["x"], dtype=np.float32)
    sh = prepare_shared(inputs)
    if "nc" not in _NC_CACHE:
        _NC_CACHE["nc"] = build_program()
    nc = _NC_CACHE["nc"]
    in_maps = []
    for b in range(NCORES):
        m = dict(sh)
        m["x"] = np.ascontiguousarray(x[b])
        in_maps.append(m)
    res = run_bass_kernel_spmd(nc, in_maps, core_ids=list(range(NCORES)))
    return np.stack([np.asarray(res.results[b]["out"], dtype=np.float32) for b in range(NCORES)], axis=0)
```
